# Optimizing a Trainium2 kernel written in Bass

```python
import math
import jax, jax.numpy as jnp
from jax import lax
import numpy as np

D_MODEL = 1024
BATCH = 8
SEQ = 2048
DEPTH = 1
DEC_BATCH = 128
DEC_SEQ = 1
PAST_LEN = 16384
PAGE_SIZE = 128

P_DIM = 256
M_HEADS = 4
M_WIDTH = D_MODEL
M_HEAD_DIM = M_WIDTH // M_HEADS
C_WIDTH = D_MODEL
CONV_W = 3
D_FF = (11 * D_MODEL) // 4
CHUNK = 64
LN_EPS = 1e-5
ALPHA = (2 * DEPTH) ** 0.25
BETA = (8 * DEPTH) ** -0.25
F_BIAS_LO = 3.0
F_BIAS_HI = 6.0

Q0 = 0
K0 = Q0 + M_WIDTH
V0 = K0 + M_WIDTH
O0 = V0 + M_WIDTH
I0 = O0 + M_WIDTH
F0 = I0 + M_HEADS
CB0 = F0 + M_HEADS
CC0 = CB0 + C_WIDTH
CH0 = CC0 + C_WIDTH
GA0 = CH0 + C_WIDTH
GB0 = GA0 + D_MODEL
N_IN = GB0 + D_MODEL

kernel_name = "hybrid_mlstm_shortconv_deepnorm_step"


def layer_norm(x, g, b):
    x32 = x.astype(jnp.float32)
    mu = x32.mean(-1, keepdims=True)
    var = jnp.square(x32 - mu).mean(-1, keepdims=True)
    out = (x32 - mu) * lax.rsqrt(var + LN_EPS) * g.astype(jnp.float32) + b.astype(jnp.float32)
    return out.astype(x.dtype)


def head_norm(h, g):
    mu = h.mean(-1, keepdims=True)
    var = jnp.square(h - mu).mean(-1, keepdims=True)
    return (h - mu) * lax.rsqrt(var + LN_EPS) * g.astype(jnp.float32)


def swiglu(x, wi, wo):
    g, u = jnp.split(x @ wi, 2, axis=-1)
    return (jax.nn.silu(g) * u) @ wo


def mlstm_chunkwise(q, k, v, log_i, log_f, C0, n0, m0):
    B, L, H, Dh = q.shape
    lc = math.gcd(L, CHUNK)
    nc = L // lc

    def to_chunks(a):
        a = a.reshape((B, nc, lc, H) + a.shape[3:])
        return jnp.moveaxis(a, (1, 3), (0, 2))

    tril = jnp.tril(jnp.ones((lc, lc), dtype=bool))

    def body(carry, inp):
        C, n, m = carry
        qc, kc, vc, li, lf = inp
        b = jnp.cumsum(lf, axis=-1)
        dmat = b[..., :, None] - b[..., None, :] + li[..., None, :]
        dmat = jnp.where(tril, dmat, float('-inf'))
        inter = b + m[..., None]
        m_row = jnp.maximum(inter, dmat.max(-1))
        w_inter = jnp.exp(inter - m_row)
        s = jnp.einsum('bhtd,bhsd->bhts', qc, kc) * jnp.exp(dmat - m_row[..., None])
        num = (w_inter[..., None] * jnp.einsum('bhtd,bhde->bhte', qc, C)
               + jnp.einsum('bhts,bhse->bhte', s, vc))
        den = w_inter * jnp.einsum('bhtd,bhd->bht', qc, n) + s.sum(-1)
        h = num / jnp.maximum(jnp.abs(den), jnp.exp(-m_row))[..., None]
        b_end = b[..., -1]
        dec = b_end[..., None] - b + li
        m_new = jnp.maximum(b_end + m, dec.max(-1))
        w_c = jnp.exp(b_end + m - m_new)
        kw = kc * jnp.exp(dec - m_new[..., None])[..., None]
        C_new = w_c[..., None, None] * C + jnp.einsum('bhsd,bhse->bhde', kw, vc)
        n_new = w_c[..., None] * n + kw.sum(-2)
        return (C_new, n_new, m_new), h

    init = (C0.astype(jnp.float32), n0.astype(jnp.float32), m0.astype(jnp.float32))
    (C, n, m), h = lax.scan(body, init, (to_chunks(q), to_chunks(k), to_chunks(v),
                                         to_chunks(log_i), to_chunks(log_f)))
    h = jnp.moveaxis(h, (0, 2), (1, 3)).reshape(B, L, H, Dh)
    return h, C, n, m


def short_conv(pre, buf, w, b):
    L = pre.shape[1]
    full = jnp.concatenate([buf.astype(pre.dtype), pre], axis=1)
    y = b
    for j in range(CONV_W):
        y = y + full[:, j:j + L] * w[j]
    return y, full[:, L:]


def decoder_layer(x, p, C0, n0, m0, conv_buf, w_in, b_in, m_norm_g, w_a, w_b, conv_w, conv_b,
                  w_mix, ffn1_wi, ffn1_wo, ffn2_wi, ffn2_wo, w_pg, w_pp, ln_g, ln_b):
    Bn, L, _ = x.shape
    f32 = jnp.float32
    x = layer_norm(ALPHA * x + 0.5 * swiglu(x, ffn1_wi, ffn1_wo), ln_g[0], ln_b[0])
    z = x @ w_in + b_in
    hs = (Bn, L, M_HEADS, M_HEAD_DIM)
    q = z[..., Q0:K0].reshape(hs).astype(f32)
    k = z[..., K0:V0].reshape(hs).astype(f32) * (M_HEAD_DIM ** -0.5)
    v = z[..., V0:O0].reshape(hs).astype(f32)
    log_i = z[..., I0:F0].astype(f32)
    log_f = jax.nn.log_sigmoid(z[..., F0:CB0].astype(f32))
    h, C, n, m = mlstm_chunkwise(q, k, v, log_i, log_f, C0, n0, m0)
    hn = head_norm(h, m_norm_g).reshape(Bn, L, M_WIDTH).astype(x.dtype)
    y_a = (hn * jax.nn.sigmoid(z[..., O0:I0])) @ w_a
    pre = z[..., CC0:CH0] * z[..., CH0:GA0]
    conv, new_buf = short_conv(pre, conv_buf, conv_w, conv_b)
    y_b = (z[..., CB0:CC0] * conv) @ w_b
    mix = (jax.nn.sigmoid(z[..., GA0:GB0]) * y_a + jax.nn.sigmoid(z[..., GB0:N_IN]) * y_b) @ w_mix
    x = layer_norm(ALPHA * x + mix, ln_g[1], ln_b[1])
    x = layer_norm(ALPHA * x + 0.5 * swiglu(x, ffn2_wi, ffn2_wo), ln_g[2], ln_b[2])
    x = layer_norm(ALPHA * x + jax.nn.sigmoid(x @ w_pg) * (p @ w_pp), ln_g[3], ln_b[3])
    return x, C, n, m, new_buf


def setup_inputs(seed: int = 0) -> dict:
    key = jax.random.key(seed)
    ks = jax.random.split(key, 32)

    def nrm(k, shape, scale):
        return jax.random.normal(k, shape, jnp.float32) * scale

    b_in = nrm(ks[10], (DEPTH, N_IN), 0.02)
    f_bias = jnp.linspace(F_BIAS_LO, F_BIAS_HI, M_HEADS, dtype=jnp.float32)[None, :] + nrm(ks[11], (DEPTH, M_HEADS), 0.1)
    b_in = b_in.at[:, F0:CB0].set(f_bias)
    return {
        "x_prompt": nrm(ks[0], (BATCH, SEQ, D_MODEL), 1.0),
        "x_sample": nrm(ks[1], (DEC_BATCH, DEC_SEQ, D_MODEL), 1.0),
        "p_prompt": nrm(ks[2], (DEPTH, BATCH, SEQ, P_DIM), 1.0),
        "p_sample": nrm(ks[3], (DEPTH, DEC_BATCH, DEC_SEQ, P_DIM), 1.0),
        "state_C": nrm(ks[4], (DEPTH, DEC_BATCH, M_HEADS, M_HEAD_DIM, M_HEAD_DIM), M_HEAD_DIM ** -0.5),
        "state_n": nrm(ks[5], (DEPTH, DEC_BATCH, M_HEADS, M_HEAD_DIM), M_HEAD_DIM ** -0.5),
        "state_m": nrm(ks[6], (DEPTH, DEC_BATCH, M_HEADS), 1.0),
        "state_conv": nrm(ks[7], (DEPTH, DEC_BATCH, CONV_W - 1, C_WIDTH), 1.0),
        "w_in": nrm(ks[8], (DEPTH, D_MODEL, N_IN), D_MODEL ** -0.5),
        "b_in": b_in,
        "m_norm_g": 1.0 + nrm(ks[12], (DEPTH, M_HEADS, M_HEAD_DIM), 0.02),
        "w_a": nrm(ks[13], (DEPTH, M_WIDTH, D_MODEL), M_WIDTH ** -0.5),
        "w_b": nrm(ks[14], (DEPTH, C_WIDTH, D_MODEL), C_WIDTH ** -0.5),
        "conv_w": nrm(ks[15], (DEPTH, CONV_W, C_WIDTH), CONV_W ** -0.5),
        "conv_b": nrm(ks[16], (DEPTH, C_WIDTH), 0.02),
        "w_mix": nrm(ks[17], (DEPTH, D_MODEL, D_MODEL), BETA * D_MODEL ** -0.5),
        "ffn1_wi": nrm(ks[18], (DEPTH, D_MODEL, 2 * D_FF), D_MODEL ** -0.5),
        "ffn1_wo": nrm(ks[19], (DEPTH, D_FF, D_MODEL), BETA * D_FF ** -0.5),
        "ffn2_wi": nrm(ks[20], (DEPTH, D_MODEL, 2 * D_FF), D_MODEL ** -0.5),
        "ffn2_wo": nrm(ks[21], (DEPTH, D_FF, D_MODEL), BETA * D_FF ** -0.5),
        "w_pg": nrm(ks[22], (DEPTH, D_MODEL, D_MODEL), D_MODEL ** -0.5),
        "w_pp": nrm(ks[23], (DEPTH, P_DIM, D_MODEL), BETA * P_DIM ** -0.5),
        "ln_g": 1.0 + nrm(ks[24], (DEPTH, 4, D_MODEL), 0.02),
        "ln_b": nrm(ks[25], (DEPTH, 4, D_MODEL), 0.02),
    }


def reference(x_prompt, x_sample, p_prompt, p_sample, state_C, state_n, state_m, state_conv,
              w_in, b_in, m_norm_g, w_a, w_b, conv_w, conv_b, w_mix,
              ffn1_wi, ffn1_wo, ffn2_wi, ffn2_wo, w_pg, w_pp, ln_g, ln_b):
    xp, xs = x_prompt, x_sample
    nb = x_prompt.shape[0]
    Cp_l, np_l, mp_l, bp_l = [], [], [], []
    Cs_l, ns_l, ms_l, bs_l = [], [], [], []
    for i in range(DEPTH):
        wl = (w_in[i], b_in[i], m_norm_g[i], w_a[i], w_b[i], conv_w[i], conv_b[i], w_mix[i],
              ffn1_wi[i], ffn1_wo[i], ffn2_wi[i], ffn2_wo[i], w_pg[i], w_pp[i], ln_g[i], ln_b[i])
        C0 = jnp.zeros((nb, M_HEADS, M_HEAD_DIM, M_HEAD_DIM), jnp.float32)
        n0 = jnp.zeros((nb, M_HEADS, M_HEAD_DIM), jnp.float32)
        m0 = jnp.zeros((nb, M_HEADS), jnp.float32)
        buf0 = jnp.zeros((nb, CONV_W - 1, C_WIDTH), xp.dtype)
        xp, Cp, np_, mp, bp = decoder_layer(xp, p_prompt[i], C0, n0, m0, buf0, *wl)
        xs, Cs, ns, ms, bs = decoder_layer(xs, p_sample[i], state_C[i], state_n[i], state_m[i],
                                           state_conv[i], *wl)
        Cp_l.append(Cp); np_l.append(np_); mp_l.append(mp); bp_l.append(bp)
        Cs_l.append(Cs); ns_l.append(ns); ms_l.append(ms); bs_l.append(bs)
    C_prompt = jnp.stack(Cp_l)
    n_prompt = jnp.stack(np_l)
    m_prompt = jnp.stack(mp_l)
    conv_prompt = jnp.stack(bp_l)
    C_sample = jnp.stack(Cs_l)
    n_sample = jnp.stack(ns_l)
    m_sample = jnp.stack(ms_l)
    conv_sample = jnp.stack(bs_l)
    return (xp, xs, C_prompt, n_prompt, m_prompt, conv_prompt, C_sample, n_sample, m_sample, conv_sample)
```

```python
import numpy as np
import concourse.bass as bass
import concourse.mybir as mybir
from concourse.bass_utils import run_bass_kernel_spmd

F32 = mybir.dt.float32
F32R = mybir.dt.float32r
BF16 = mybir.dt.bfloat16
AF = mybir.ActivationFunctionType
ALU = mybir.AluOpType

D = 1024
SEQ = 2048
NS = 16
PD = 256
DFF = 2816
NJ = 22
NIN = 9224
Q0, K0, V0, O0, I0, F0, CB0, CC0, CH0, GA0, GB0 = 0, 1024, 2048, 3072, 4096, 4100, 4104, 5128, 6152, 7176, 8200
ALPHA = 2.0 ** 0.25
IA = 1.0 / ALPHA
LN_EPS = 1e-5
EPS_EFF = LN_EPS / (ALPHA * ALPHA)
PW = 512
NPASS = 4
EPOCH = 12000
import os
FORCE_INC = bool(int(os.environ.get('FORCE_INC', '0')))
DBG = os.environ.get('DBG', '')
USE_SCR = False
USE_WARM = True
WK_LN, WK_ML, WK_IN = 22, 20, 24
LN_XR_ENG = 'act'


class Buf:
    __slots__ = ("name", "ap", "w", "r", "dsem", "dcnt", "excl", "arena")

    def __init__(self, name, ap=None, excl=False, arena=False):
        self.arena = arena
        self.name = name
        self.ap = ap
        self.excl = excl
        self.w = []
        self.r = []
        self.dsem = None
        self.dcnt = 0


class Sched:
    def __init__(self, nc):
        self.nc = nc
        self.eng = {"pe": nc.tensor, "act": nc.scalar, "dve": nc.vector, "pool": nc.gpsimd, "sp": nc.sync}
        self.sem = {}
        self.cnt = {}
        self.known = {e: {} for e in self.eng}
        self.nsem = 0
        self.ninst = {e: 0 for e in self.eng}
        self.nwait = {e: 0 for e in self.eng}
        self.arena_toks = []
        self.last_barrier = []
        for e in self.eng:
            self._new_epoch(e)

    def _alloc_sem(self, name):
        self.nsem += 1
        return self.nc.alloc_semaphore(f"{name}_{self.nsem}")

    def _new_epoch(self, e):
        self.sem[e] = self._alloc_sem(f"s_{e}")
        self.cnt[e] = 0

    def _wait(self, e, deps):
        best = {}
        for (s, v) in deps:
            k = id(s)
            if k not in best or best[k][1] < v:
                best[k] = (s, v)
        kn = self.known[e]
        for k, (s, v) in best.items():
            if kn.get(k, 0) >= v:
                continue
            self.eng[e].wait_ge(s, v)
            self.nwait[e] += 1
            kn[k] = v

    def op(self, e, fn, reads=(), writes=(), inc=True, skip_own=False):
        deps = []
        own = self.sem[e]
        for b in reads:
            deps.extend(b.w)
            if b.excl:
                deps.extend(t for t in b.r if t[0] is not own)
        for b in writes:
            deps.extend(b.w)
            deps.extend(b.r)
        if skip_own:
            deps = [d for d in deps if d[0] is not own]
        self._wait(e, deps)
        inst = fn(self.eng[e])
        self.ninst[e] += 1
        if FORCE_INC:
            inc = True
        if inc:
            self.cnt[e] += 1
            inst.then_inc(own, 1)
            tok = (own, self.cnt[e])
        else:
            tok = (own, self.cnt[e] + 1)
        for b in reads:
            b.r.append(tok)
        for b in writes:
            b.w = [tok]
            b.r = []
        if inc and self.cnt[e] >= EPOCH:
            self._new_epoch(e)
        return tok

    def dma(self, q, out_ap, in_ap, reads=(), writes=(), join=False, owner=None, **kw):
        deps = []
        for b in reads:
            deps.extend(b.w)
        for b in writes:
            if not join:
                deps.extend(b.w)
            deps.extend(b.r)
        self._wait(q, deps)
        if owner is None:
            owner = writes[0] if writes else reads[0]
        if owner.dsem is None:
            owner.dsem = self._alloc_sem("d")
        inst = self.eng[q].dma_start(out=out_ap, in_=in_ap, **kw)
        owner.dcnt += 16
        inst.then_inc(owner.dsem, 16)
        tok = (owner.dsem, owner.dcnt)
        self.ninst[q] += 1
        if any(b.arena for b in reads) or any(b.arena for b in writes):
            self.arena_toks.append(tok)
        for b in reads:
            b.r.append(tok)
        for b in writes:
            if join:
                b.w = [t for t in b.w if t[0] is not owner.dsem] + [tok]
            else:
                b.w = [tok]
            b.r = []
        return tok

    def barrier(self, engines=("pe", "act", "dve")):
        toks = [(self.sem[e], self.cnt[e]) for e in engines if self.cnt[e] > 0] + self.arena_toks
        self.arena_toks = []
        for e in engines:
            self._wait(e, toks)
        self.last_barrier = [(self.sem[e], self.cnt[e]) for e in engines if self.cnt[e] > 0]

    def finish(self, bufs, e="sp"):
        deps = []
        for b in bufs:
            deps.extend(b.w)
            deps.extend(b.r)
        self._wait(e, deps)


def build_program(with_samples=True, stage=99, npass=NPASS):
    nc = bass.Bass("TRN2", target_bir_lowering=False)
    S = Sched(nc)

    def din(name, shape):
        return nc.dram_tensor(name, list(shape), F32, kind="ExternalInput").ap()

    def dout(name, shape):
        return nc.dram_tensor(name, list(shape), F32, kind="ExternalOutput").ap()

    x_d = din("x", [SEQ, D]); xs_d = din("xs", [NS, D])
    p_d = din("p", [SEQ, PD]); psm_d = din("psm", [NS, PD])
    sC_d = din("sC", [NS, 4, 256, 256]); sn_d = din("sn", [NS * 4 * 2, 128]); sm_d = din("sm", [NS, 4])
    scv_d = din("scv", [NS * 2, D])
    w_in_d = din("w_in", [D, NIN]); b_in_d = din("b_in", [1, NIN])
    mng_d = din("m_norm_g", [8, 128])
    w_a_d = din("w_a", [D, D]); w_b_d = din("w_b", [D, D])
    cw_d = din("conv_w", [24, 128]); cb_d = din("conv_b", [8, 128])
    w_mix_d = din("w_mix", [D, D])
    f1i_d = din("ffn1_wi", [D, 2 * DFF]); f1o_d = din("ffn1_wo", [DFF, D])
    f2i_d = din("ffn2_wi", [D, 2 * DFF]); f2o_d = din("ffn2_wo", [DFF, D])
    w_pg_d = din("w_pg", [D, D]); w_pp_d = din("w_pp", [PD, D])
    lng_d = din("ln_g", [32, 128]); lnb_d = din("ln_b", [32, 128])
    cst_d = din("consts", [128, 768])

    y_d = dout("y", [SEQ, D]); ys_d = dout("ys", [NS, D])
    Cp_d = dout("Cp", [4, 256, 256]); np_d = dout("np", [8, 128]); mp_d = dout("mp", [4, 1])
    cvp_d = dout("cvp", [2, D])
    Cs_d = dout("Cs", [NS, 4, 256, 256]); ns_d = dout("ns", [NS, D]); ms_d = dout("ms", [NS, 4])
    cvs_d = dout("cvs", [NS, 2, D])
    out_bufs = []

    def dbuf(name):
        b = Buf(name)
        return b

    wsrc = Buf("wsrc")
    def sb(name, shape, dt=F32):
        return nc.alloc_sbuf_tensor(name, list(shape), dt).ap()

    NTM = PW + NS
    cst = Buf("cst", sb("cst", [128, 768]))
    ident = cst.ap[:, 0:128]
    maskadd4 = cst.ap[:, 128:640]
    eye4 = cst.ap[0:4, 640:644]
    xf_t = sb("xf", [128, 8, NTM]); xb_t = sb("xb", [128, 8, NTM], BF16)
    xf = [Buf(f"xf{i}", xf_t[:, i, :]) for i in range(8)]
    xb = [Buf(f"xb{i}", xb_t[:, i, :]) for i in range(8)]
    xin = [Buf(f"xin{i}", sb(f"xin{i}", [128, D])) for i in range(2)]
    pin = [Buf(f"pin{i}", sb(f"pin{i}", [128, PD])) for i in range(2)]
    pT_t = sb("pT", [128, 2, NTM], BF16)
    pT = [Buf(f"pT{i}", pT_t[:, i, :]) for i in range(2)]
    ring = [Buf(f"ring{i}", sb(f"ring{i}", [128, 4096], BF16)) for i in range(4)]
    ring_i = [0]

    NSL = 96
    wscr = nc.dram_tensor("wscr", [NSL, 128, 4096], BF16).ap()
    scr = [Buf(f"scr{i}") for i in range(NSL)]
    slot_idx = [0]
    cur_idx = {}
    pass_no = [0]

    def next_slot():
        s = ring[ring_i[0] % 4]
        ring_i[0] += 1
        cur_idx[s.name] = slot_idx[0]
        slot_idx[0] += 1
        assert slot_idx[0] <= NSL
        return s

    Cst = Buf("Cst", sb("Cst", [128, 4, 2, 257]))
    Cb0_ = Buf("Cb0", sb("Cb0", [128, 4, 2, 258], BF16)); Cb = [Cb0_, Cb0_]
    cols = Buf("cols", sb("cols", [128, 192]))
    bcol = cols.ap[:, 0:72]; cwc = cols.ap[:, 72:96]; cbc = cols.ap[:, 96:104]; mngc = cols.ap[:, 104:112]
    lngc = cols.ap[:, 112:144]; lnbc = cols.ap[:, 144:176]; bk16 = cols.ap[:, 176:184]
    rows = Buf("rows", sb("rows", [128, 128]))
    smallc = Buf("smallc", sb("smallc", [128, 8]))
    epsc = smallc.ap[:, 0:1]; onec = smallc.ap[:, 1:2]; bic = smallc.ap[0:4, 2:3]; bfc = smallc.ap[0:4, 3:4]
    eps5c = smallc.ap[:, 4:5]
    ones_r = Buf("ones_r", sb("ones_r", [128, 128], F32R))
    ones4 = Buf("ones4", sb("ones4", [4, 128]))
    wg = Buf("wg", sb("wg", [128, 8, 8], BF16))
    Bc = Buf("Bc", sb("Bc", [4, 513])); Gt = Buf("Gt", sb("Gt", [4, 513]))
    ccar = Buf("ccar", sb("ccar", [128, 8, 2]))
    colf = Buf("colf", sb("colf", [128, 4, 12]))
    Gb = Buf("Gb", sb("Gb", [128, 5, 4]))
    sm4 = Buf("sm4", sb("sm4", [128, 4, 4, 4]))
    wcb = sm4.ap[:, 0]; ds16 = sm4.ap[:, 1]; winter = sm4.ap[:, 2]; nrm = sm4.ap[:, 3]
    st8 = Buf("st8", sb("st8", [128, 64]))
    sq_p = [Buf(f"sq{i}", sb(f"sq{i}", [128, NTM], F32R)) for i in range(2)]
    xr_p = [Buf(f"xr{i}", sb(f"xr{i}", [128, NTM], F32R)) for i in range(2)]
    pres_all = Buf("pres_all", sb("pres_all", [128, 8, NS]))
    cbufT = Buf("cbufT", sb("cbufT", [128, 8, 2 * NS]))
    ASZ = 102400
    arena_t = sb("arena", [128, ASZ // 4])
    ar_off = [0]

    def carve(name, shape, dt=F32):
        n = int(np.prod(shape[1:]))
        nb = n * (4 if dt in (F32, F32R) else 2)
        nb4 = (nb + 3) // 4
        o = ar_off[0]
        assert (o + nb4) * 4 <= ASZ, (name, o * 4, nb)
        ar_off[0] = o + nb4 + (-(nb4) % 8)
        AR_HW[0] = max(AR_HW[0], ar_off[0] * 4)
        ap = arena_t[0:shape[0], o:o + nb4]
        if dt != F32:
            ap = ap.bitcast(dt)
        if dt == BF16:
            ap = ap[:, 0:n]
        if len(shape) == 3:
            ap = ap.rearrange("p (a b) -> p a b", a=shape[1])
        elif len(shape) == 4:
            ap = ap.rearrange("p (a b c) -> p a b c", a=shape[1], b=shape[2])
        b_ = Buf(name, ap, arena=True)
        b_.w = list(S.last_barrier)
        return b_

    def arena_reset(mark=0):
        S.barrier()
        ar_off[0] = mark

    banks = [Buf(f"bank{i}", nc.alloc_psum_tensor(f"bank{i}", [128, 512], F32).ap(), excl=True) for i in range(8)]
    bank_i = [0]

    reserved = set()

    def nb():
        while True:
            b = banks[bank_i[0] % 8]
            bank_i[0] += 1
            if b.name not in reserved:
                return b

    def mm(out_ap, lhsT, rhs, start, stop, reads, wbuf, inc=None):
        if inc is None:
            inc = stop
        S.op("pe", lambda e: e.matmul(out_ap, lhsT=lhsT, rhs=rhs, start=start, stop=stop),
             reads=reads, writes=[wbuf], inc=inc, skip_own=True)

    def tr(out_ap, in_ap, idn, reads, wbuf, inc=True):
        S.op("pe", lambda e: e.transpose(out_ap, in_ap, idn), reads=reads, writes=[wbuf], inc=inc, skip_own=True)

    dmy_w = Buf("dmy_w", sb("dmy_w", [128, 128], BF16)); dmy_x = Buf("dmy_x", sb("dmy_x", [128, 512], BF16))

    def warm(K):
        if K <= 0 or not USE_WARM:
            return
        bk = nb()
        for _ in range(K):
            S.op("pe", lambda e: e.matmul(bk.ap, lhsT=dmy_w.ap, rhs=dmy_x.ap, start=True, stop=True),
                 reads=[dmy_w, dmy_x], writes=[bk], inc=False, skip_own=True)

    def act(out, in_, func, reads, writes, bias=None, scale=None):
        kw = {}
        if bias is not None:
            kw["bias"] = bias
        if scale is not None:
            kw["scale"] = scale
        S.op("act", lambda e: e.activation(out=out, in_=in_, func=func, **kw), reads=reads, writes=writes)

    def tt(out, in0, in1, op, reads, writes, eng="dve"):
        S.op(eng, lambda e: e.tensor_tensor(out=out, in0=in0, in1=in1, op=op), reads=reads, writes=writes)

    def stt(out, in0, scalar, in1, op0, op1, reads, writes):
        S.op("dve", lambda e: e.scalar_tensor_tensor(out=out, in0=in0, scalar=scalar, in1=in1, op0=op0, op1=op1),
             reads=reads, writes=writes)

    def ts(out, in0, s1, s2, op0, op1, reads, writes, eng="dve"):
        S.op(eng, lambda e: e.tensor_scalar(out=out, in0=in0, scalar1=s1, scalar2=s2, op0=op0, op1=op1),
             reads=reads, writes=writes)

    def cp(eng, out, in_, reads, writes):
        if eng == "act":
            act(out, in_, AF.Copy, reads, writes)
        else:
            S.op(eng, lambda e: e.tensor_copy(out=out, in_=in_), reads=reads, writes=writes)

    def wload(dst_buf, dst_ap, src_ap, join=False, view=None):
        if view is None or not USE_SCR:
            if view is not None:
                dst_ap = view(dst_buf.ap)
            S.dma("pool", dst_ap, src_ap, reads=[wsrc], writes=[dst_buf], join=join)
            return
        i = cur_idx[dst_buf.name]
        if pass_no[0] == 0:
            S.dma("pool", view(dst_buf.ap), src_ap, reads=[wsrc], writes=[dst_buf], join=join)
            S.dma("sp", view(wscr[i]), view(dst_buf.ap), reads=[dst_buf], writes=[scr[i]], join=True, owner=dst_buf)
        else:
            S.dma("pool", view(dst_buf.ap), view(wscr[i]), reads=[scr[i]], writes=[dst_buf], join=join)

    V8 = lambda a: a.rearrange("p (kc n) -> p kc n", kc=8)

    def wview(dram, c0, ncol):
        return dram.rearrange("(kc p) n -> p kc n", p=128)[:, :, c0:c0 + ncol]

    S.dma("sp", cst.ap, cst_d, reads=[wsrc], writes=[cst])
    S.dma("sp", rows.ap[0:32, :], b_in_d[0, 0:4096].rearrange("(r c) -> r c", c=128), reads=[wsrc], writes=[rows])
    S.dma("sp", rows.ap[32:72, :], b_in_d[0, CB0:NIN].rearrange("(r c) -> r c", c=128), reads=[wsrc], writes=[rows], join=True)
    S.dma("sp", rows.ap[72:96, :], cw_d, reads=[wsrc], writes=[rows], join=True)
    S.dma("sp", rows.ap[96:104, :], cb_d, reads=[wsrc], writes=[rows], join=True)
    S.dma("sp", rows.ap[104:112, :], mng_d, reads=[wsrc], writes=[rows], join=True)
    b0 = nb()
    tr(b0.ap[:, 0:112], rows.ap[0:112, :], ident[0:112, 0:112], [rows, cst], b0)
    cp("dve", cols.ap[:, 0:112], b0.ap[:, 0:112], [b0], [cols])
    rows2 = Buf("rows2", sb("rows2", [64, 128]))
    S.dma("sp", rows2.ap[0:32, :], lng_d, reads=[wsrc], writes=[rows2])
    S.dma("sp", rows2.ap[32:64, :], lnb_d, reads=[wsrc], writes=[rows2], join=True)
    b0 = nb()
    tr(b0.ap[:, 0:64], rows2.ap[0:64, :], ident[0:64, 0:64], [rows2, cst], b0)
    cp("dve", cols.ap[:, 112:176], b0.ap[:, 0:64], [b0], [cols])
    S.op("dve", lambda e: e.tensor_scalar(out=bk16, in0=bcol[:, 8:16], scalar1=1.0 / 16.0, scalar2=None, op0=ALU.mult),
         reads=[cols], writes=[cols])
    S.op("dve", lambda e: e.memset(smallc.ap[:, 0:1], EPS_EFF), writes=[smallc])
    S.op("dve", lambda e: e.memset(smallc.ap[:, 1:2], 1.0), reads=[], writes=[smallc])
    S.op("dve", lambda e: e.memset(smallc.ap[:, 4:5], LN_EPS), reads=[], writes=[smallc])
    with nc.allow_non_contiguous_dma(reason="tiny gate-bias columns"):
        S.dma("sp", bic, b_in_d[0, I0:I0 + 4].rearrange("(p o) -> p o", o=1), reads=[wsrc], writes=[smallc], join=True)
        S.dma("sp", bfc, b_in_d[0, F0:F0 + 4].rearrange("(p o) -> p o", o=1), reads=[wsrc], writes=[smallc], join=True)
        wload(wg, wg.ap, wview(w_in_d, I0, 8))
    tmp1 = Buf("tmp1", sb("tmp1", [128, 128]))
    S.op("dve", lambda e: e.memset(tmp1.ap, 1.0), writes=[tmp1])
    cp("dve", ones_r.ap, tmp1.ap, [tmp1], [ones_r])
    cp("dve", ones4.ap, tmp1.ap[0:4, :], [tmp1], [ones4])
    S.op("dve", lambda e: e.memset(Cst.ap, 0.0), writes=[Cst])
    S.op("dve", lambda e: e.memset(dmy_w.ap, 0.0), writes=[dmy_w])
    S.op("dve", lambda e: e.memset(dmy_x.ap, 0.0), writes=[dmy_x])
    S.op("dve", lambda e: e.memset(Cb[0].ap, 0.0), writes=[Cb[0]])
    S.op("dve", lambda e: e.memset(Bc.ap[:, 0:1], 0.0), writes=[Bc])
    S.op("dve", lambda e: e.memset(Gt.ap[:, 0:1], 0.0), writes=[Gt])
    S.op("dve", lambda e: e.memset(ccar.ap, 0.0), writes=[ccar])

    if stage == -1:
        ob = Buf("dbg")
        S.dma("sp", y_d[0:128, 0:192], cols.ap, reads=[cols, smallc, wg, ones_r, ones4, Cst, Bc, Gt, ccar], writes=[ob])
        S.finish([ob], "sp")
        return nc, S
    def proj(slot_ap, ncol_chunks, rhs_list, ctiles, slot_buf, evac, kcn=8, col0=0):
        for ti, (n0, n) in enumerate(ctiles):
            bks = [nb() for _ in range(ncol_chunks)]
            for kc in range(kcn):
                for ci in range(ncol_chunks):
                    mm(bks[ci].ap[:, 0:n], slot_ap[:, kc, col0 + ci * 128: col0 + (ci + 1) * 128], rhs_list[kc].ap[:, n0:n0 + n],
                       kc == 0, kc == kcn - 1, [slot_buf, rhs_list[kc]], bks[ci])
            for ci in range(ncol_chunks):
                evac(ci, ti, (n0, n), bks[ci])

    def ln_begin(ctiles):
        act(smallc.ap[:, 5:6], smallc.ap[:, 1:2], AF.Sqrt, [smallc], [smallc])

    def ln_acc(fc, ti, nn):
        pass

    LN_BASE = (ASZ - 5 * NTM * 4) // 4
    _o = ar_off[0]
    ar_off[0] = LN_BASE
    ln_mean = carve("mean", [128, NTM]); ln_var = carve("var", [128, NTM]); ln_rstd = carve("rstd", [128, NTM])
    ln_u = [carve(f"u{i}", [128, NTM]) for i in range(2)]
    ar_off[0] = _o
    AR_HW[0] = 0

    def layer_norm(ln, ctiles, NT):
        sq = sq_p; xr = xr_p
        mean, var, rstd, u = ln_mean, ln_var, ln_rstd, ln_u
        for ti, (n0, n) in enumerate(ctiles):
            s1 = nb(); s2 = nb()
            for fc in range(8):
                q_ = sq[fc % 2]; r_ = xr[fc % 2]
                tt(q_.ap[:, 0:n], xf[fc].ap[:, n0:n0 + n], xf[fc].ap[:, n0:n0 + n], ALU.mult, [xf[fc]], [q_])
                cp(LN_XR_ENG, r_.ap[:, 0:n], xf[fc].ap[:, n0:n0 + n], [xf[fc]], [r_])
                mm(s1.ap[:, 0:n], ones_r.ap, r_.ap[:, 0:n], fc == 0, fc == 7, [ones_r, r_], s1, inc=True)
                mm(s2.ap[:, 0:n], ones_r.ap, q_.ap[:, 0:n], fc == 0, fc == 7, [ones_r, q_], s2, inc=True)
            sl = slice(n0, n0 + n)
            warm(WK_LN if n >= 256 else 4)
            act(rstd.ap[:, sl], s1.ap[:, 0:n], AF.Square, [s1], [rstd], scale=1.0 / D)
            stt(var.ap[:, sl], s2.ap[:, 0:n], 1.0 / D, rstd.ap[:, sl], ALU.mult, ALU.subtract, [s2, rstd], [var])
            act(mean.ap[:, sl], s1.ap[:, 0:n], AF.Copy, [s1], [mean], scale=1.0 / D)
            act(var.ap[:, sl], var.ap[:, sl], AF.Sqrt, [var], [var], bias=epsc)
            for fc in range(2):
                tt(u[fc].ap[:, 0:n], xf[fc].ap[:, sl], mean.ap[:, sl], ALU.subtract, [xf[fc], mean], [u[fc]])
            S.op("dve", lambda e: e.reciprocal(out=rstd.ap[:, sl], in_=var.ap[:, sl]), reads=[var], writes=[rstd])
            for fc in range(8):
                u_ = u[fc % 2]
                if fc >= 2:
                    tt(u_.ap[:, 0:n], xf[fc].ap[:, sl], mean.ap[:, sl], ALU.subtract, [xf[fc], mean], [u_])
                tt(u_.ap[:, 0:n], u_.ap[:, 0:n], rstd.ap[:, sl], ALU.mult, [u_, rstd], [u_])
                gcol = lngc[:, ln * 8 + fc: ln * 8 + fc + 1]; bcl = lnbc[:, ln * 8 + fc: ln * 8 + fc + 1]
                act(xb[fc].ap[:, sl], u_.ap[:, 0:n], AF.Identity, [u_, cols], [xb[fc]], bias=bcl, scale=gcol)
                act(xf[fc].ap[:, sl], u_.ap[:, 0:n], AF.Identity, [u_, cols], [xf[fc]], bias=bcl, scale=gcol)

    def ffn(wi_d, wo_d, ctiles, NT):
        mark = ar_off[0]
        actb = [carve(f"act{j}", [128, NTM], BF16) for j in range(NJ)]
        sg = [carve(f"sg{i}", [128, NTM]) for i in range(2)]
        k = 0
        for jp in range(NJ // 2):
            sl = next_slot()
            v = sl.ap.rearrange("p (kc n) -> p kc n", kc=8)
            wload(sl, None, wview(wi_d, jp * 256, 256), view=lambda a: a.rearrange("p (kc n) -> p kc n", kc=8)[:, :, 0:256])
            wload(sl, None, wview(wi_d, DFF + jp * 256, 256), join=True, view=lambda a: a.rearrange("p (kc n) -> p kc n", kc=8)[:, :, 256:512])
            for (n0, n) in ctiles:
                bks = [nb() for _ in range(4)]
                offs = [0, 256, 128, 384]
                for kc in range(8):
                    for q4 in range(4):
                        mm(bks[q4].ap[:, 0:n], v[:, kc, offs[q4]:offs[q4] + 128], xb[kc].ap[:, n0:n0 + n], kc == 0, kc == 7, [sl, xb[kc]], bks[q4])
                for jj in range(2):
                    j = jp * 2 + jj
                    gb_ = bks[2 * jj]; ub_ = bks[2 * jj + 1]
                    s_ = sg[k % 2]; k += 1
                    act(s_.ap[:, 0:n], gb_.ap[:, 0:n], AF.Silu, [gb_], [s_])
                    tt(actb[j].ap[:, n0:n0 + n], s_.ap[:, 0:n], ub_.ap[:, 0:n], ALU.mult, [s_, ub_], [actb[j]])
        ln_begin(ctiles)
        for dc in range(8):
            sl = next_slot()
            v = sl.ap[:, 0:NJ * 128].rearrange("p (j n) -> p j n", j=NJ)
            wload(sl, None, wo_d.rearrange("(j p) n -> p j n", p=128)[:, :, dc * 128:(dc + 1) * 128], view=lambda a: a[:, 0:NJ * 128].rearrange("p (j n) -> p j n", j=NJ))
            for ti, (n0, n) in enumerate(ctiles):
                bk = nb()
                for j in range(NJ):
                    mm(bk.ap[:, 0:n], v[:, j, :], actb[j].ap[:, n0:n0 + n], j == 0, j == NJ - 1, [sl, actb[j]], bk)
                stt(xf[dc].ap[:, n0:n0 + n], bk.ap[:, 0:n], 0.5 * IA, xf[dc].ap[:, n0:n0 + n], ALU.mult, ALU.add, [bk, xf[dc]], [xf[dc]])
                ln_acc(dc, ti, (n0, n))
        arena_reset(mark)

    for ps_i in range(npass):
        last = (ps_i == NPASS - 1)
        slot_idx[0] = 0
        pass_no[0] = ps_i
        use_s = last and with_samples
        NT = PW + (NS if use_s else 0)
        ctiles = [(0, PW)] + ([(PW, NS)] if use_s else [])
        t0 = ps_i * PW
        if ps_i > 0:
            warm(WK_IN)
        for tt_i in range(4):
            xi = xin[tt_i % 2]; pi = pin[tt_i % 2]
            S.dma("sp", xi.ap, x_d[t0 + tt_i * 128: t0 + (tt_i + 1) * 128, :], reads=[wsrc], writes=[xi])
            S.dma("sp", pi.ap, p_d[t0 + tt_i * 128: t0 + (tt_i + 1) * 128, :], reads=[wsrc], writes=[pi])
            for half in range(2):
                bk = nb()
                for f4 in range(4):
                    fc = half * 4 + f4
                    tr(bk.ap[:, f4 * 128:(f4 + 1) * 128], xi.ap[:, fc * 128:(fc + 1) * 128], ident, [xi, cst], bk, inc=(f4 == 3))
                dst = xf_t[:, half * 4:(half + 1) * 4, tt_i * 128:(tt_i + 1) * 128]
                dstb = xb_t[:, half * 4:(half + 1) * 4, tt_i * 128:(tt_i + 1) * 128]
                src = bk.ap.rearrange("p (a b) -> p a b", a=4)
                grp = xf[half * 4:(half + 1) * 4]; grpb = xb[half * 4:(half + 1) * 4]
                cp("act", dst, src, [bk], grp)
                if 'nobf' not in DBG:
                    cp("dve", dstb, src, [bk], grpb)
            if 'nop' in DBG:
                continue
            bk = nb()
            for kc in range(2):
                tr(bk.ap[:, kc * 128:(kc + 1) * 128], pi.ap[:, kc * 128:(kc + 1) * 128], ident, [pi, cst], bk, inc=(kc == 1))
            cp("dve", pT_t[:, :, tt_i * 128:(tt_i + 1) * 128], bk.ap[:, 0:256].rearrange("p (a b) -> p a b", a=2), [bk], pT)
        if use_s:
            xi = xin[0]; pi = pin[0]
            S.dma("sp", xi.ap[0:NS, :], xs_d, reads=[wsrc], writes=[xi])
            S.dma("sp", pi.ap[0:NS, :], psm_d, reads=[wsrc], writes=[pi])
            for half in range(2):
                bk = nb()
                for f4 in range(4):
                    fc = half * 4 + f4
                    tr(bk.ap[:, f4 * NS:(f4 + 1) * NS], xi.ap[0:NS, fc * 128:(fc + 1) * 128], ident[0:NS, 0:NS], [xi, cst], bk, inc=(f4 == 3))
                src = bk.ap[:, 0:4 * NS].rearrange("p (a b) -> p a b", a=4)
                cp("act", xf_t[:, half * 4:(half + 1) * 4, PW:PW + NS], src, [bk], xf[half * 4:(half + 1) * 4])
                cp("dve", xb_t[:, half * 4:(half + 1) * 4, PW:PW + NS], src, [bk], xb[half * 4:(half + 1) * 4])
            bk = nb()
            for kc in range(2):
                tr(bk.ap[:, kc * NS:(kc + 1) * NS], pi.ap[0:NS, kc * 128:(kc + 1) * 128], ident[0:NS, 0:NS], [pi, cst], bk, inc=(kc == 1))
            cp("dve", pT_t[:, :, PW:PW + NS], bk.ap[:, 0:2 * NS].rearrange("p (a b) -> p a b", a=2), [bk], pT)

        if stage == -2:
            ob = Buf("dbg")
            S.dma("sp", y_d[0:128, 0:512], xf_t[:, 0, 0:512], reads=xf + xb + pT, writes=[ob])
            S.finish([ob], "sp")
            return nc, S
        if stage >= 1:
            ffn(f1i_d, f1o_d, ctiles, NT)
        if stage >= 2:
            layer_norm(0, ctiles, NT)

        if stage >= 3:
            mark2 = ar_off[0]
            qT = [carve(f"qT{i}", [128, NTM], BF16) for i in range(8)]
            kT = [carve(f"kT{i}", [128, NTM], BF16) for i in range(8)]
            kw = carve("kw", [128, 4, 1024], BF16)
            vaug = carve("vaug", [128, 4, 4, 258], BF16)
            sigo = [carve(f"sigo{i}", [128, NTM]) for i in range(8)]
            yain = [carve(f"yain{i}", [128, NTM], BF16) for i in range(8)]
            G1 = carve("G1", [4, NTM]); G2 = carve("G2", [4, NTM]); G3 = carve("G3", [4, NTM]); G4 = carve("G4", [4, NTM])
            mark2b = ar_off[0]
            for (n0, n) in ctiles:
                gi = nb(); gf = nb()
                for kc in range(8):
                    mm(gi.ap[0:4, 0:n], wg.ap[:, kc, 0:4], xb[kc].ap[:, n0:n0 + n], kc == 0, kc == 7, [wg, xb[kc]], gi)
                    mm(gf.ap[0:4, 0:n], wg.ap[:, kc, 4:8], xb[kc].ap[:, n0:n0 + n], kc == 0, kc == 7, [wg, xb[kc]], gf)
                act(G1.ap[:, n0:n0 + n], gi.ap[0:4, 0:n], AF.Identity, [gi, smallc], [G1], bias=bic)
                act(G2.ap[:, n0:n0 + n], gf.ap[0:4, 0:n], AF.Identity, [gf, smallc], [G2], bias=bfc)
            act(G3.ap[:, 0:NT], G2.ap[:, 0:NT], AF.Abs, [G2], [G3])
            act(G3.ap[:, 0:NT], G3.ap[:, 0:NT], AF.Exp, [G3], [G3], scale=-1.0)
            act(G3.ap[:, 0:NT], G3.ap[:, 0:NT], AF.Ln, [G3, smallc], [G3], bias=onec[0:4, :])
            S.op("dve", lambda e: e.tensor_scalar_min(out=G4.ap[:, 0:NT], in0=G2.ap[:, 0:NT], scalar1=0.0), reads=[G2], writes=[G4])
            tt(G2.ap[:, 0:NT], G4.ap[:, 0:NT], G3.ap[:, 0:NT], ALU.subtract, [G4, G3], [G2])
            S.op("dve", lambda e: e.memset(G4.ap[:, 0:PW], 1.0), writes=[G4])
            S.op("dve", lambda e: e.tensor_tensor_scan(out=Bc.ap[:, 1:1 + PW], data0=G4.ap[:, 0:PW], data1=G2.ap[:, 0:PW],
                                                        initial=Bc.ap[:, 0:1], op0=ALU.mult, op1=ALU.add), reads=[G4, G2, Bc], writes=[Bc])
            tt(G3.ap[:, 0:PW], G1.ap[:, 0:PW], Bc.ap[:, 1:1 + PW], ALU.subtract, [G1, Bc], [G3])
            S.op("dve", lambda e: e.tensor_tensor_scan(out=Gt.ap[:, 1:1 + PW], data0=G3.ap[:, 0:PW], data1=G3.ap[:, 0:PW],
                                                        initial=Gt.ap[:, 0:1], op0=ALU.max, op1=ALU.max), reads=[G3, Gt], writes=[Gt])
            tt(G4.ap[:, 0:PW], Bc.ap[:, 1:1 + PW], Gt.ap[:, 1:1 + PW], ALU.add, [Bc, Gt], [G4])
            for g2 in range(2):
                sl = next_slot(); v = sl.ap.rearrange("p (kc n) -> p kc n", kc=8)
                wload(sl, None, wview(w_in_d, Q0 + g2 * 512, 512), view=V8)

                def ev_q(ci, ti, nn, bk, g2=g2):
                    c = g2 * 4 + ci
                    act(qT[c].ap[:, nn[0]:nn[0] + nn[1]], bk.ap[:, 0:nn[1]], AF.Identity, [bk, cols], [qT[c]], bias=bcol[:, c:c + 1])
                proj(v, 4, xb, ctiles, sl, ev_q)
            for c in range(4):
                bk = nb()
                cs = slice(c * 128, (c + 1) * 128)
                tr(bk.ap[:, 0:4], G3.ap[0:4, cs], ident[0:4, 0:4], [G3, cst], bk, inc=False)
                tr(bk.ap[:, 4:8], Gt.ap[0:4, 1 + c * 128: 1 + (c + 1) * 128], ident[0:4, 0:4], [Gt, cst], bk, inc=False)
                tr(bk.ap[:, 8:12], G4.ap[0:4, cs], ident[0:4, 0:4], [G4, cst], bk, inc=True)
                cp("act", colf.ap[:, c, :], bk.ap[:, 0:12], [bk], [colf])
            Bm5 = carve("Bm5", [4, 5, 4])
            tt(Bm5.ap, Gt.ap[0:4, 0:513:128].unsqueeze(2).to_broadcast([4, 5, 4]), eye4.unsqueeze(1).to_broadcast([4, 5, 4]),
               ALU.mult, [Gt, cst], [Bm5])
            bk = nb()
            mm(bk.ap[:, 0:20], ones4.ap, Bm5.ap.rearrange("p a b -> p (a b)"), True, True, [ones4, Bm5], bk)
            cp("act", Gb.ap.rearrange("p a b -> p (a b)"), bk.ap[:, 0:20], [bk], [Gb])
            tt(wcb, Gb.ap[:, 0:4, :], Gb.ap[:, 1:5, :], ALU.subtract, [Gb], [sm4])
            tt(ds16, colf.ap[:, :, 0:4], Gb.ap[:, 1:5, :], ALU.subtract, [colf, Gb], [sm4])
            tt(winter, Gb.ap[:, 0:4, :], colf.ap[:, :, 4:8], ALU.subtract, [Gb, colf], [sm4])
            S.op("dve", lambda e: e.tensor_scalar(out=nrm, in0=colf.ap[:, :, 8:12], scalar1=-1.0, scalar2=None, op0=ALU.mult), reads=[colf], writes=[sm4])
            act(sm4.ap.rearrange("p a b c -> p (a b c)"), sm4.ap.rearrange("p a b c -> p (a b c)"), AF.Exp, [sm4], [sm4])
            S.op("dve", lambda e: e.tensor_scalar(out=ds16, in0=ds16, scalar1=1.0 / 16.0, scalar2=None, op0=ALU.mult), reads=[sm4], writes=[sm4])

            S.op("dve", lambda e: e.memset(vaug.ap[:, :, :, 256:258], 1.0), writes=[vaug])
            bias_hl = carve("bias_hl", [1, 2, 2048], BF16)
            ones_bf = carve("ones_bf", [1, 128], BF16)
            mark2c = ar_off[0]
            bias_f = carve("bias_f", [1, 1024])
            bias_t = carve("bias_t", [1, 1024])
            for hb in range(2):
                hs_ = slice(hb * 1024, (hb + 1) * 1024)
                S.dma("sp", bias_f.ap, b_in_d[0:1, K0 + hb * 1024:K0 + (hb + 1) * 1024], reads=[wsrc], writes=[bias_f])
                cp("dve", bias_hl.ap[:, 0, hs_], bias_f.ap, [bias_f], [bias_hl])
                cp("dve", bias_t.ap, bias_hl.ap[:, 0, hs_], [bias_hl], [bias_t])
                tt(bias_t.ap, bias_f.ap, bias_t.ap, ALU.subtract, [bias_f, bias_t], [bias_t])
                cp("dve", bias_hl.ap[:, 1, hs_], bias_t.ap, [bias_t], [bias_hl])
            S.op("dve", lambda e: e.memset(ones_bf.ap, 1.0), writes=[ones_bf])
            S.last_barrier = list(S.last_barrier) + bias_f.w + bias_f.r + bias_t.w + bias_t.r
            ar_off[0] = mark2c
            vTs = carve("vTs", [128, 8, NS])
            for g2 in range(4):
                sl = next_slot(); v = sl.ap.rearrange("p (kc n) -> p kc n", kc=8)
                wload(sl, None, wview(w_in_d, K0 + g2 * 512, 512), view=V8)
                if g2 < 2:
                    def ev_k(ci, ti, nn, bk, g2=g2):
                        c = g2 * 4 + ci
                        act(kT[c].ap[:, nn[0]:nn[0] + nn[1]], bk.ap[:, 0:nn[1]], AF.Identity, [bk, cols], [kT[c]],
                            bias=bk16[:, c:c + 1], scale=1.0 / 16.0)
                    proj(v, 4, xb, ctiles, sl, ev_k)
                for c in range(4):
                    bk = nb()
                    cs = slice(c * 128, (c + 1) * 128)
                    for kc in range(8):
                        mm(bk.ap, xb[kc].ap[:, cs], v[:, kc, :], kc == 0, False, [sl, xb[kc]], bk, inc=False)
                    mm(bk.ap, ones_bf.ap, bias_hl.ap[:, 0, g2 * 512:(g2 + 1) * 512], False, False, [ones_bf, bias_hl], bk, inc=False)
                    mm(bk.ap, ones_bf.ap, bias_hl.ap[:, 1, g2 * 512:(g2 + 1) * 512], False, True, [ones_bf, bias_hl], bk, inc=True)
                    for hh_ in range(2):
                        h = (g2 % 2) * 2 + hh_
                        if g2 < 2:
                            act(kw.ap[:, c, h * 256:(h + 1) * 256], bk.ap[:, hh_ * 256:(hh_ + 1) * 256], AF.Copy, [bk, sm4], [kw],
                                scale=ds16[:, c, h:h + 1])
                        else:
                            cp("dve", vaug.ap[:, c, h, 0:256], bk.ap[:, hh_ * 256:(hh_ + 1) * 256], [bk], [vaug])
                if use_s and g2 >= 2:
                    for ci in range(4):
                        c = (g2 - 2) * 4 + ci
                        bk = nb()
                        for kc in range(8):
                            mm(bk.ap[:, 0:NS], v[:, kc, ci * 128:(ci + 1) * 128], xb[kc].ap[:, PW:PW + NS], kc == 0, kc == 7, [sl, xb[kc]], bk)
                        act(vTs.ap[:, c, :], bk.ap[:, 0:NS], AF.Identity, [bk, cols], [vTs], bias=bcol[:, 16 + c:17 + c])
            for g2 in range(2):
                sl = next_slot(); v = sl.ap.rearrange("p (kc n) -> p kc n", kc=8)
                wload(sl, None, wview(w_in_d, O0 + g2 * 512, 512), view=V8)

                def ev_o(ci, ti, nn, bk, g2=g2):
                    c = g2 * 4 + ci
                    act(sigo[c].ap[:, nn[0]:nn[0] + nn[1]], bk.ap[:, 0:nn[1]], AF.Sigmoid, [bk, cols], [sigo[c]], bias=bcol[:, 24 + c:25 + c])
                proj(v, 4, xb, ctiles, sl, ev_o)

            mark_ml = ar_off[0]
            Bm2 = carve("Bm2", [4, 4, 128])
            argb = [carve(f"arg{i}", [128, 4, 128]) for i in range(2)]
            Dm = [carve(f"Dm{i}", [128, 512]) for i in range(2)]
            Stb = [carve(f"St{i}", [128, 512], BF16) for i in range(2)]
            numdb = [carve(f"numd{i}", [128, 4, 257]) for i in range(2)]
            st_dn = Buf("st_dn", st8.ap[:, 0:8]); st_rs = Buf("st_rs", st8.ap[:, 8:16]); st_bn = Buf("st_bn", st8.ap[:, 16:48])
            st_nb = Buf("st_nb", st8.ap[:, 48:52])
            st_dn.w = list(st8.w) + list(st8.r); st_rs.w = list(st_dn.w); st_bn.w = list(st_dn.w); st_nb.w = list(st_dn.w)
            dn = st8.ap[:, 0:4]; rden = st8.ap[:, 4:8]; sd = st8.ap[:, 8:12]; rs = st8.ap[:, 12:16]
            bst = st8.ap[:, 16:40].rearrange("p (h s) -> p h s", h=4)
            mv = st8.ap[:, 40:48].rearrange("p (h s) -> p h s", h=4)
            CK = {}

            def csl(c):
                return slice(c * 128, (c + 1) * 128)

            def G_dve1(c):
                tt(Bm2.ap, Gt.ap[0:4, 1 + c * 128:1 + (c + 1) * 128].unsqueeze(1).to_broadcast([4, 4, 128]),
                   eye4.unsqueeze(2).to_broadcast([4, 4, 128]), ALU.mult, [Gt, cst], [Bm2])

            def G_pe(c):
                ng = nb(); CK[("ng", c)] = ng
                mm(ng.ap, ones4.ap, Bm2.ap.rearrange("p a b -> p (a b)"), True, True, [ones4, Bm2], ng)

            def G_dve2(c):
                ng = CK[("ng", c)]; ar_ = argb[c % 2]
                stt(ar_.ap.rearrange("p a b -> p (a b)"), ng.ap, -1.0, maskadd4, ALU.mult, ALU.add, [ng, cst], [ar_])

            def G_act(c):
                ar_ = argb[c % 2]; D_ = Dm[c % 2]
                for h in range(4):
                    act(D_.ap[:, h * 128:(h + 1) * 128], ar_.ap[:, h, :], AF.Exp, [ar_, colf], [D_], bias=colf.ap[:, c, h:h + 1])

            def A_pe1(c):
                cs = csl(c); gci = ps_i * 4 + c; cbr = Cb[gci % 2]
                sp_ = nb(); CK[("sp", c)] = sp_
                for h in range(4):
                    for dc in range(2):
                        ch = 2 * h + dc
                        mm(sp_.ap[:, h * 128:(h + 1) * 128], kT[ch].ap[:, cs], qT[ch].ap[:, cs], dc == 0, dc == 1, [kT[ch], qT[ch]], sp_,
                           inc=(h == 3 and dc == 1))
                P1s = []
                for h in range(4):
                    P1 = nb(); P1s.append(P1)
                    for dc in range(2):
                        mm(P1.ap[:, 0:257], qT[2 * h + dc].ap[:, cs], cbr.ap[:, h, dc, 0:257], dc == 0, dc == 1, [qT[2 * h + dc], cbr], P1)
                CK[("P1", c)] = P1s

            def A_St(c):
                tt(Stb[c % 2].ap, CK[("sp", c)].ap, Dm[c % 2].ap, ALU.mult, [CK[("sp", c)], Dm[c % 2]], [Stb[c % 2]])

            def A_copies(c):
                nd = numdb[c % 2]
                for h in range(4):
                    P1 = CK[("P1", c)][h]
                    act(nd.ap[:, h, :], P1.ap[:, 0:257], AF.Copy, [P1, sm4], [nd], scale=winter[:, c, h:h + 1])

            def A_P2(c):
                St_ = Stb[c % 2]; P2s = []
                for h in range(4):
                    P2 = nb(); P2s.append(P2)
                    mm(P2.ap[:, 0:257], St_.ap[:, h * 128:(h + 1) * 128], vaug.ap[:, c, h, 0:257], True, True, [St_, vaug], P2)
                CK[("P2", c)] = P2s

            def A_add(c):
                nd = numdb[c % 2]
                for h in range(4):
                    P2 = CK[("P2", c)][h]
                    tt(nd.ap[:, h, :], nd.ap[:, h, :], P2.ap[:, 0:257], ALU.add, [nd, P2], [nd])

            def B_state(c):
                gci = ps_i * 4 + c; cbw = Cb[(gci + 1) % 2]
                for h in range(4):
                    for dc in range(2):
                        U = nb()
                        mm(U.ap[:, 0:257], kw.ap[:, c, h * 256 + dc * 128: h * 256 + (dc + 1) * 128], vaug.ap[:, c, h, 0:257], True, True, [kw, vaug], U)
                        stt(Cst.ap[:, h, dc, :], Cst.ap[:, h, dc, :], wcb[:, c, h:h + 1], U.ap[:, 0:257], ALU.mult, ALU.add, [Cst, sm4, U], [Cst])
                cp("act", cbw.ap[:, :, :, 0:257], Cst.ap, [Cst], [cbw])

            def T_abs(c):
                act(dn, numdb[c % 2].ap[:, :, 256], AF.Abs, [numdb[c % 2]], [st_dn])

            def T_maxrecip(c):
                tt(dn, dn, nrm[:, c, :], ALU.max, [st_dn, sm4], [st_dn])
                S.op("dve", lambda e: e.reciprocal(out=rden, in_=dn), reads=[st_dn], writes=[st_dn])

            def T_hh(c):
                nd = numdb[c % 2]
                for h in range(4):
                    act(nd.ap[:, h, 0:256], nd.ap[:, h, 0:256], AF.Copy, [nd, st_dn], [nd], scale=rden[:, h:h + 1])

            def T_bn(c):
                nd = numdb[c % 2]
                for h in range(4):
                    S.op("dve", lambda e, h=h: e.bn_stats(out=bst[:, h, :], in_=nd.ap[:, h, 0:256]), reads=[nd], writes=[st_bn])
                    S.op("dve", lambda e, h=h: e.bn_aggr(out=mv[:, h, :], in_=bst[:, h, :]), reads=[st_bn], writes=[st_bn])

            def T_sqrt(c):
                act(sd, mv[:, :, 1], AF.Sqrt, [st_bn, smallc], [st_rs], bias=eps5c)

            def T_norm(c):
                nd = numdb[c % 2]
                nbias = st8.ap[:, 48:52]
                S.op("dve", lambda e: e.reciprocal(out=rs, in_=sd), reads=[st_rs], writes=[st_rs])
                stt(nbias, mv[:, :, 0], -1.0, rs, ALU.mult, ALU.mult, [st_bn, st_rs], [st_nb])
                for h in range(4):
                    act(nd.ap[:, h, 0:256], nd.ap[:, h, 0:256], AF.Identity, [nd, st_rs, st_nb], [nd], bias=nbias[:, h:h + 1], scale=rs[:, h:h + 1])

            def T_out(c):
                nd = numdb[c % 2]; cs = csl(c)
                warm(WK_ML if c < 3 else WK_ML + 22)
                for half in range(2):
                    tb_ = nb()
                    for j4 in range(4):
                        ch = half * 4 + j4
                        h, ec = ch // 2, ch % 2
                        tr(tb_.ap[:, j4 * 128:(j4 + 1) * 128], nd.ap[:, h, ec * 128:(ec + 1) * 128], ident, [nd, cst], tb_, inc=(j4 == 3))
                    for j4 in range(4):
                        ch = half * 4 + j4
                        stt(yain[ch].ap[:, cs], tb_.ap[:, j4 * 128:(j4 + 1) * 128], mngc[:, ch:ch + 1], sigo[ch].ap[:, cs],
                            ALU.mult, ALU.mult, [tb_, cols, sigo[ch]], [yain[ch]])

            G_dve1(0); warm(12); G_pe(0); G_dve2(0); G_act(0)
            for c in range(4):
                p = c - 1
                n = c + 1 if c + 1 < 4 else None
                if p >= 0:
                    T_abs(p)
                if n is not None:
                    G_dve1(n); G_pe(n)
                if p >= 0:
                    T_maxrecip(p)
                if n is not None:
                    G_dve2(n); G_act(n)
                if p >= 0:
                    T_hh(p)
                A_pe1(c)
                A_St(c)
                A_copies(c)
                A_P2(c)
                if p >= 0:
                    T_bn(p)
                A_add(c)
                if p >= 0:
                    T_sqrt(p)
                B_state(c)
                if p >= 0:
                    T_norm(p); T_out(p)
            T_abs(3); T_maxrecip(3); T_hh(3); T_bn(3); T_sqrt(3); T_norm(3); T_out(3)
            st8.w = list(st_dn.w) + list(st_rs.w) + list(st_bn.w) + list(st_nb.w)
            st8.r = list(st_dn.r) + list(st_rs.r) + list(st_bn.r) + list(st_nb.r)
            cp("act", Bc.ap[:, 0:1], Bc.ap[:, PW:PW + 1], [Bc], [Bc])
            cp("act", Gt.ap[:, 0:1], Gt.ap[:, PW:PW + 1], [Gt], [Gt])

            if use_s:
                S.barrier()
                ar_off[0] = mark_ml
                SC = slice(PW, PW + NS)
                C0f = [carve(f"C0f{i}", [128, 8, 256]) for i in range(2)]
                C0b = Buf("C0b", kw.ap.rearrange("p a b -> p (a b)")[:, 0:2048].rearrange("p (c e) -> p c e", c=8), arena=True)
                n0T = carve("n0T", [128, 128]); rowsN = carve("rowsN", [128, 128])
                m0c = carve("m0c", [4, NS]); t_int = carve("t_int", [4, NS]); t_mn = carve("t_mn", [4, NS])
                Qs = carve("Qs", [4, 3, NS]); BmQ = carve("BmQ", [4, 3, 4, NS]); sbq = carve("sbq", [128, 3, 4, NS])
                qk = carve("qk", [128, 4, NS]); qn = carve("qn", [128, 4, NS]); s_ = carve("s_", [128, 4, NS])
                den = carve("den", [128, 4, NS]); rdn = carve("rdn", [128, 4, NS])
                mean_s = carve("mean_s", [128, 4, NS]); var_s = carve("var_s", [128, 4, NS])
                hq = carve("hq", [128, 8, NS]); numT = carve("numT", [128, 8, NS]); kws = carve("kws", [128, 8, NS])
                nnew = carve("nnew", [128, 8, NS]); t8 = carve("t8", [128, 8, NS])
                vm = carve("vm", [NS, 256])
                vtok_s = xin[0]; crows = xin[1]
                vtok_ap = xin[0].ap[0:NS, :]; crows_ap = xin[1].ap[0:2 * NS, :]
                w_int_b = sbq.ap[:, 0]; es_b = sbq.ap[:, 1]; nrm_b = sbq.ap[:, 2]
                n0v = n0T.ap.rearrange("p (b c) -> p c b", c=8)
                S.dma("sp", rowsN.ap, sn_d, reads=[wsrc], writes=[rowsN])
                bk = nb(); tr(bk.ap[:, 0:128], rowsN.ap, ident, [rowsN, cst], bk)
                cp("dve", n0T.ap, bk.ap[:, 0:128], [bk], [n0T])
                with nc.allow_non_contiguous_dma(reason="tiny"):
                    S.dma("sp", m0c.ap, sm_d.rearrange("b h -> h b"), reads=[wsrc], writes=[m0c])
                S.dma("sp", crows_ap, scv_d, reads=[wsrc], writes=[crows])
                for half in range(2):
                    bk = nb()
                    for f4 in range(4):
                        fc = half * 4 + f4
                        tr(bk.ap[:, f4 * 32:(f4 + 1) * 32], crows_ap[:, fc * 128:(fc + 1) * 128], ident[0:32, 0:32], [crows, cst], bk, inc=(f4 == 3))
                    cp("dve", cbufT.ap[:, half * 4:(half + 1) * 4, :], bk.ap[:, 0:128].rearrange("p (a b) -> p a b", a=4), [bk], [cbufT])
                ob = dbuf("cvs0")
                S.dma("sp", cvs_d[:, 0, :], scv_d.rearrange("(b j) d -> b j d", j=2)[:, 1, :], reads=[wsrc], writes=[ob]); out_bufs.append(ob)
                for half in range(2):
                    bk = nb()
                    for f4 in range(4):
                        c = half * 4 + f4
                        tr(bk.ap[0:NS, f4 * 128:(f4 + 1) * 128], vTs.ap[:, c, :], ident, [vTs, cst], bk, inc=(f4 == 3))
                    cp("dve", vtok_ap[:, half * 512:(half + 1) * 512], bk.ap[0:NS, :], [bk], [vtok_s])
                tt(t_int.ap, G2.ap[:, SC], m0c.ap, ALU.add, [G2, m0c], [t_int])
                tt(t_mn.ap, t_int.ap, G1.ap[:, SC], ALU.max, [t_int, G1], [t_mn])
                tt(Qs.ap[:, 0, :], t_int.ap, t_mn.ap, ALU.subtract, [t_int, t_mn], [Qs])
                tt(Qs.ap[:, 1, :], G1.ap[:, SC], t_mn.ap, ALU.subtract, [G1, t_mn], [Qs])
                S.op("dve", lambda e: e.tensor_scalar(out=Qs.ap[:, 2, :], in0=t_mn.ap, scalar1=-1.0, scalar2=None, op0=ALU.mult), reads=[t_mn], writes=[Qs])
                act(Qs.ap.rearrange("p a b -> p (a b)"), Qs.ap.rearrange("p a b -> p (a b)"), AF.Exp, [Qs], [Qs])
                with nc.allow_non_contiguous_dma(reason="tiny"):
                    ob = dbuf("msd"); S.dma("sp", ms_d.rearrange("b h -> h b"), t_mn.ap, reads=[t_mn], writes=[ob]); out_bufs.append(ob)
                tt(BmQ.ap, Qs.ap.unsqueeze(2).to_broadcast([4, 3, 4, NS]), eye4.unsqueeze(1).unsqueeze(3).to_broadcast([4, 3, 4, NS]),
                   ALU.mult, [Qs, cst], [BmQ])
                bk = nb()
                mm(bk.ap[:, 0:192], ones4.ap, BmQ.ap.rearrange("p a b c -> p (a b c)"), True, True, [ones4, BmQ], bk)
                cp("act", sbq.ap.rearrange("p a b c -> p (a b c)"), bk.ap[:, 0:192], [bk], [sbq])
                pq = sq_p[0]; pn = sq_p[1]
                for ch in range(8):
                    tt(pq.ap[:, ch * NS:(ch + 1) * NS], qT[ch].ap[:, SC], kT[ch].ap[:, SC], ALU.mult, [qT[ch], kT[ch]], [pq])
                    tt(pn.ap[:, ch * NS:(ch + 1) * NS], qT[ch].ap[:, SC], n0v[:, ch, :], ALU.mult, [qT[ch], n0T], [pn])
                b1 = nb(); b2 = nb()
                mm(b1.ap[:, 0:128], ones_r.ap, pq.ap[:, 0:128], True, True, [ones_r, pq], b1)
                mm(b2.ap[:, 0:128], ones_r.ap, pn.ap[:, 0:128], True, True, [ones_r, pn], b2)
                v1 = b1.ap[:, 0:128].rearrange("p (h d b) -> p h d b", h=4, d=2)
                v2 = b2.ap[:, 0:128].rearrange("p (h d b) -> p h d b", h=4, d=2)
                cp("act", qk.ap, v1[:, :, 0, :], [b1], [qk])
                tt(qk.ap, qk.ap, v1[:, :, 1, :], ALU.add, [qk, b1], [qk])
                cp("act", qn.ap, v2[:, :, 0, :], [b2], [qn])
                tt(qn.ap, qn.ap, v2[:, :, 1, :], ALU.add, [qn, b2], [qn])
                tt(s_.ap, qk.ap, es_b, ALU.mult, [qk, sbq], [s_])
                tt(den.ap, qn.ap, w_int_b, ALU.mult, [qn, sbq], [den])
                tt(den.ap, den.ap, s_.ap, ALU.add, [den, s_], [den])
                act(den.ap, den.ap, AF.Abs, [den], [den])
                tt(den.ap, den.ap, nrm_b, ALU.max, [den, sbq], [den])
                S.op("dve", lambda e: e.reciprocal(out=rdn.ap, in_=den.ap), reads=[den], writes=[rdn])
                for ch in range(8):
                    h = ch // 2
                    tt(kws.ap[:, ch, :], kT[ch].ap[:, SC], es_b[:, h, :], ALU.mult, [kT[ch], sbq], [kws])
                    tt(nnew.ap[:, ch, :], n0v[:, ch, :], w_int_b[:, h, :], ALU.mult, [n0T, sbq], [nnew])
                tt(nnew.ap, nnew.ap, kws.ap, ALU.add, [nnew, kws], [nnew])
                nsrows = xin[1]; nsrows_ap = xin[1].ap[0:NS, :]
                for half in range(2):
                    bk = nb()
                    for f4 in range(4):
                        c = half * 4 + f4
                        tr(bk.ap[0:NS, f4 * 128:(f4 + 1) * 128], nnew.ap[:, c, :], ident, [nnew, cst], bk, inc=(f4 == 3))
                    cp("dve", nsrows_ap[:, half * 512:(half + 1) * 512], bk.ap[0:NS, :], [bk], [nsrows])
                ob = dbuf("nsd"); S.dma("sp", ns_d, nsrows_ap, reads=[nsrows], writes=[ob]); out_bufs.append(ob)
                hqb = nb(); reserved.add(hqb.name)
                cfh = [[Buf(f"cfh{i}{k}", C0f[i].ap[:, 4 * k:4 * k + 4, :], arena=True) for k in range(2)] for i in range(2)]
                for i in range(2):
                    for k in range(2):
                        cfh[i][k].w = list(C0f[i].w)
                C0bh = [Buf(f"C0b{k}", C0b.ap[:, 4 * k:4 * k + 4, :], arena=True) for k in range(2)]
                for k in range(2):
                    C0bh[k].w = list(C0b.w)

                def load_c0(b_, k):
                    cf_ = cfh[b_ % 2][k]
                    S.dma("sp", cf_.ap.rearrange("p (h d) e -> p h d e", h=2), sC_d[b_, 2 * k:2 * k + 2].rearrange("h (d p) e -> p h d e", p=128),
                          reads=[wsrc], writes=[cf_])

                def vm_loc(bb, h):
                    p0 = 32 * (h // 2)
                    if bb % 2 == 0:
                        return pin[h % 2], pin[h % 2].ap[p0:p0 + NS, :], p0
                    return xin[1], xin[1].ap[p0:p0 + NS, (h % 2) * 256:(h % 2 + 1) * 256], p0

                def emit_vb(bb):
                    vbs_ = [nb(), nb()]
                    for h in range(4):
                        buf_, ap_, p0 = vm_loc(bb, h)
                        ts(ap_, vtok_ap[:, h * 256:(h + 1) * 256], ident[0:NS, bb:bb + 1], None, ALU.mult, ALU.bypass, [vtok_s, cst], [buf_])
                    for h in range(4):
                        buf_, ap_, p0 = vm_loc(bb, h)
                        vb = vbs_[h // 2]
                        mm(vb.ap[:, (h % 2) * 256:(h % 2 + 1) * 256], tmp1.ap[p0:p0 + NS, :], ap_, True, True, [tmp1, buf_], vb)
                    return vbs_

                load_c0(0, 0); load_c0(0, 1)
                vb_next = emit_vb(0)
                for b in range(NS):
                    if b + 1 < NS:
                        load_c0(b + 1, 0); load_c0(b + 1, 1)
                    vbs = vb_next
                    for k in range(2):
                        cf = cfh[b % 2][k]; cb_ = C0bh[k]
                        cp("act", cb_.ap, cf.ap, [cf], [cb_])
                        for hh_ in range(2):
                            h = 2 * k + hh_
                            for ec in range(2):
                                col = (h * 2 + ec) * NS + b
                                for dc in range(2):
                                    mm(hqb.ap[:, col:col + 1], cb_.ap[:, hh_ * 2 + dc, ec * 128:(ec + 1) * 128], qT[h * 2 + dc].ap[:, PW + b:PW + b + 1],
                                       dc == 0, dc == 1, [cb_, qT[h * 2 + dc]], hqb, inc=(dc == 1 and ec == 1 and hh_ == 1))
                        if k == 0 and b + 1 < NS:
                            vb_next = emit_vb(b + 1)
                        for hh_ in range(2):
                            h = 2 * k + hh_
                            act(cf.ap[:, 2 * hh_:2 * hh_ + 2, :], cf.ap[:, 2 * hh_:2 * hh_ + 2, :], AF.Copy, [cf, sbq], [cf], scale=w_int_b[:, h, b:b + 1])
                        vb = vbs[k]
                        for hh_ in range(2):
                            h = 2 * k + hh_
                            for dc in range(2):
                                ch = h * 2 + dc
                                stt(cf.ap[:, hh_ * 2 + dc, :], vb.ap[:, hh_ * 256:(hh_ + 1) * 256], kws.ap[:, ch, b:b + 1], cf.ap[:, hh_ * 2 + dc, :], ALU.mult, ALU.add, [vb, kws, cf], [cf])
                        ob = dbuf(f"Csd{b}_{k}")
                        S.dma("sp", Cs_d[b, 2 * k:2 * k + 2].rearrange("h (d p) e -> p h d e", p=128), cf.ap.rearrange("p (h d) e -> p h d e", h=2),
                              reads=[cf], writes=[ob]); out_bufs.append(ob)
                cp("act", hq.ap.rearrange("p a b -> p (a b)"), hqb.ap[:, 0:128], [hqb], [hq])
                reserved.discard(hqb.name)
                for ch in range(8):
                    h = ch // 2
                    tt(numT.ap[:, ch, :], hq.ap[:, ch, :], w_int_b[:, h, :], ALU.mult, [hq, sbq], [numT])
                    tt(t8.ap[:, ch, :], vTs.ap[:, ch, :], s_.ap[:, h, :], ALU.mult, [vTs, s_], [t8])
                tt(numT.ap, numT.ap, t8.ap, ALU.add, [numT, t8], [numT])
                for ch in range(8):
                    tt(numT.ap[:, ch, :], numT.ap[:, ch, :], rdn.ap[:, ch // 2, :], ALU.mult, [numT, rdn], [numT])
                hr = xr_p[0]; hs = xr_p[1]
                cp("dve", hr.ap[:, 0:128], numT.ap.rearrange("p a b -> p (a b)"), [numT], [hr])
                tt(hs.ap[:, 0:128], numT.ap.rearrange("p a b -> p (a b)"), numT.ap.rearrange("p a b -> p (a b)"), ALU.mult, [numT], [hs])
                b1 = nb(); b2 = nb()
                mm(b1.ap[:, 0:128], ones_r.ap, hr.ap[:, 0:128], True, True, [ones_r, hr], b1)
                mm(b2.ap[:, 0:128], ones_r.ap, hs.ap[:, 0:128], True, True, [ones_r, hs], b2)
                v1 = b1.ap[:, 0:128].rearrange("p (h d b) -> p h d b", h=4, d=2)
                v2 = b2.ap[:, 0:128].rearrange("p (h d b) -> p h d b", h=4, d=2)
                cp("act", mean_s.ap, v1[:, :, 0, :], [b1], [mean_s])
                tt(mean_s.ap, mean_s.ap, v1[:, :, 1, :], ALU.add, [mean_s, b1], [mean_s])
                cp("act", var_s.ap, v2[:, :, 0, :], [b2], [var_s])
                tt(var_s.ap, var_s.ap, v2[:, :, 1, :], ALU.add, [var_s, b2], [var_s])
                S.op("dve", lambda e: e.tensor_scalar(out=mean_s.ap, in0=mean_s.ap, scalar1=1.0 / 256.0, scalar2=None, op0=ALU.mult), reads=[mean_s], writes=[mean_s])
                tt(qk.ap, mean_s.ap, mean_s.ap, ALU.mult, [mean_s], [qk])
                stt(var_s.ap, var_s.ap, 1.0 / 256.0, qk.ap, ALU.mult, ALU.subtract, [var_s, qk], [var_s])
                act(var_s.ap, var_s.ap, AF.Sqrt, [var_s, smallc], [var_s], bias=eps5c)
                S.op("dve", lambda e: e.reciprocal(out=var_s.ap, in_=var_s.ap), reads=[var_s], writes=[var_s])
                for ch in range(8):
                    h = ch // 2
                    tt(numT.ap[:, ch, :], numT.ap[:, ch, :], mean_s.ap[:, h, :], ALU.subtract, [numT, mean_s], [numT])
                    tt(numT.ap[:, ch, :], numT.ap[:, ch, :], var_s.ap[:, h, :], ALU.mult, [numT, var_s], [numT])
                    stt(yain[ch].ap[:, SC], numT.ap[:, ch, :], mngc[:, ch:ch + 1], sigo[ch].ap[:, SC], ALU.mult, ALU.mult,
                        [numT, cols, sigo[ch]], [yain[ch]])

            S.barrier()
            ar_off[0] = mark2b
            ybin = [Buf(f"ybin{i}", kT[i].ap) for i in range(8)]
            mmix = [Buf(f"mmix{i}", qT[i].ap) for i in range(8)]
            Cc = [carve(f"Cc{i}", [128, NTM]) for i in range(2)]
            pre = [carve(f"pre{i}", [128, NTM + 2]) for i in range(2)]
            acc = [carve(f"acc{i}", [128, NTM]) for i in range(2)]
            for g2 in range(2):
                sC_ = next_slot(); vC = sC_.ap.rearrange("p (kc n) -> p kc n", kc=8)
                wload(sC_, None, wview(w_in_d, CC0 + g2 * 512, 512), view=V8)
                sH_ = next_slot(); vH = sH_.ap.rearrange("p (kc n) -> p kc n", kc=8)
                wload(sH_, None, wview(w_in_d, CH0 + g2 * 512, 512), view=V8)
                sB_ = next_slot(); vB = sB_.ap.rearrange("p (kc n) -> p kc n", kc=8)
                wload(sB_, None, wview(w_in_d, CB0 + g2 * 512, 512), view=V8)
                for ci in range(4):
                    fc = g2 * 4 + ci
                    Cc_ = Cc[fc % 2]; pre_ = pre[fc % 2]; acc_ = acc[fc % 2]
                    cp("act", pre_.ap[:, 0:2], ccar.ap[:, fc, :], [ccar], [pre_])
                    for (n0, n) in ctiles:
                        bC = nb(); bH = nb(); bB = nb()
                        for kc in range(8):
                            mm(bC.ap[:, 0:n], vC[:, kc, ci * 128:(ci + 1) * 128], xb[kc].ap[:, n0:n0 + n], kc == 0, kc == 7, [sC_, xb[kc]], bC)
                        for kc in range(8):
                            mm(bH.ap[:, 0:n], vH[:, kc, ci * 128:(ci + 1) * 128], xb[kc].ap[:, n0:n0 + n], kc == 0, kc == 7, [sH_, xb[kc]], bH)
                        for kc in range(8):
                            mm(bB.ap[:, 0:n], vB[:, kc, ci * 128:(ci + 1) * 128], xb[kc].ap[:, n0:n0 + n], kc == 0, kc == 7, [sB_, xb[kc]], bB)
                        act(Cc_.ap[:, n0:n0 + n], bC.ap[:, 0:n], AF.Identity, [bC, cols], [Cc_], bias=bcol[:, 40 + fc:41 + fc])
                        stt(pre_.ap[:, 2 + n0:2 + n0 + n], bH.ap[:, 0:n], bcol[:, 48 + fc:49 + fc], Cc_.ap[:, n0:n0 + n], ALU.add, ALU.mult,
                            [bH, cols, Cc_], [pre_])
                        if n0 == 0:
                            act(acc_.ap[:, 0:n], pre_.ap[:, 2:2 + n], AF.Identity, [pre_, cols], [acc_], bias=cbc[:, fc:fc + 1], scale=cwc[:, 16 + fc:17 + fc])
                            stt(acc_.ap[:, 0:n], pre_.ap[:, 1:1 + n], cwc[:, 8 + fc:9 + fc], acc_.ap[:, 0:n], ALU.mult, ALU.add, [pre_, cols, acc_], [acc_])
                            stt(acc_.ap[:, 0:n], pre_.ap[:, 0:n], cwc[:, fc:fc + 1], acc_.ap[:, 0:n], ALU.mult, ALU.add, [pre_, cols, acc_], [acc_])
                            cp("act", ccar.ap[:, fc, :], pre_.ap[:, n:n + 2], [pre_], [ccar])
                        else:
                            ps_ = pre_.ap[:, 2 + n0:2 + n0 + n]
                            act(acc_.ap[:, n0:n0 + n], ps_, AF.Identity, [pre_, cols], [acc_], bias=cbc[:, fc:fc + 1], scale=cwc[:, 16 + fc:17 + fc])
                            cbv = cbufT.ap[:, fc, :].rearrange("p (b j) -> p j b", j=2)
                            stt(acc_.ap[:, n0:n0 + n], cbv[:, 1, :], cwc[:, 8 + fc:9 + fc], acc_.ap[:, n0:n0 + n], ALU.mult, ALU.add, [cbufT, cols, acc_], [acc_])
                            stt(acc_.ap[:, n0:n0 + n], cbv[:, 0, :], cwc[:, fc:fc + 1], acc_.ap[:, n0:n0 + n], ALU.mult, ALU.add, [cbufT, cols, acc_], [acc_])
                            cp("act", pres_all.ap[:, fc, :], ps_, [pre_], [pres_all])
                        stt(ybin[fc].ap[:, n0:n0 + n], bB.ap[:, 0:n], bcol[:, 32 + fc:33 + fc], acc_.ap[:, n0:n0 + n], ALU.add, ALU.mult,
                            [bB, cols, acc_], [ybin[fc]])
            S.barrier()
            ar_off[0] = mark2b
            sga = [carve(f"sga{i}", [128, NTM]) for i in range(2)]
            ta = [carve(f"ta{i}", [128, NTM]) for i in range(2)]
            tb = [carve(f"tb{i}", [128, NTM]) for i in range(2)]
            for g4 in range(4):
                s1_ = next_slot(); v1 = s1_.ap.rearrange("p (w kc n) -> p w kc n", w=2, kc=8)
                wload(s1_, None, wview(w_in_d, GA0 + g4 * 256, 256), view=lambda a: a.rearrange("p (w kc n) -> p w kc n", w=2, kc=8)[:, 0])
                wload(s1_, None, wview(w_a_d, g4 * 256, 256), join=True, view=lambda a: a.rearrange("p (w kc n) -> p w kc n", w=2, kc=8)[:, 1])
                s2_ = next_slot(); v2 = s2_.ap.rearrange("p (w kc n) -> p w kc n", w=2, kc=8)
                wload(s2_, None, wview(w_in_d, GB0 + g4 * 256, 256), view=lambda a: a.rearrange("p (w kc n) -> p w kc n", w=2, kc=8)[:, 0])
                wload(s2_, None, wview(w_b_d, g4 * 256, 256), join=True, view=lambda a: a.rearrange("p (w kc n) -> p w kc n", w=2, kc=8)[:, 1])
                for ci in range(2):
                    dc = g4 * 2 + ci
                    for (n0, n) in ctiles:
                        k_ = dc % 2
                        bga = nb(); bya = nb(); bgb = nb(); byb = nb()
                        for kc in range(8):
                            mm(bga.ap[:, 0:n], v1[:, 0, kc, ci * 128:(ci + 1) * 128], xb[kc].ap[:, n0:n0 + n], kc == 0, kc == 7, [s1_, xb[kc]], bga)
                        for kc in range(8):
                            mm(bya.ap[:, 0:n], v1[:, 1, kc, ci * 128:(ci + 1) * 128], yain[kc].ap[:, n0:n0 + n], kc == 0, kc == 7, [s1_, yain[kc]], bya)
                        for kc in range(8):
                            mm(bgb.ap[:, 0:n], v2[:, 0, kc, ci * 128:(ci + 1) * 128], xb[kc].ap[:, n0:n0 + n], kc == 0, kc == 7, [s2_, xb[kc]], bgb)
                        for kc in range(8):
                            mm(byb.ap[:, 0:n], v2[:, 1, kc, ci * 128:(ci + 1) * 128], ybin[kc].ap[:, n0:n0 + n], kc == 0, kc == 7, [s2_, ybin[kc]], byb)
                        act(sga[k_].ap[:, 0:n], bga.ap[:, 0:n], AF.Sigmoid, [bga, cols], [sga[k_]], bias=bcol[:, 56 + dc:57 + dc])
                        tt(ta[k_].ap[:, 0:n], sga[k_].ap[:, 0:n], bya.ap[:, 0:n], ALU.mult, [sga[k_], bya], [ta[k_]])
                        act(sga[k_].ap[:, 0:n], bgb.ap[:, 0:n], AF.Sigmoid, [bgb, cols], [sga[k_]], bias=bcol[:, 64 + dc:65 + dc])
                        tt(tb[k_].ap[:, 0:n], sga[k_].ap[:, 0:n], byb.ap[:, 0:n], ALU.mult, [sga[k_], byb], [tb[k_]])
                        tt(mmix[dc].ap[:, n0:n0 + n], ta[k_].ap[:, 0:n], tb[k_].ap[:, 0:n], ALU.add, [ta[k_], tb[k_]], [mmix[dc]])
            ln_begin(ctiles)
            for g2 in range(2):
                sl = next_slot(); v = sl.ap.rearrange("p (kc n) -> p kc n", kc=8)
                wload(sl, None, wview(w_mix_d, g2 * 512, 512), view=V8)

                def ev_m(ci, ti, nn, bk, g2=g2):
                    dc = g2 * 4 + ci
                    sl_ = slice(nn[0], nn[0] + nn[1])
                    stt(xf[dc].ap[:, sl_], bk.ap[:, 0:nn[1]], IA, xf[dc].ap[:, sl_], ALU.mult, ALU.add, [bk, xf[dc]], [xf[dc]])
                    ln_acc(dc, ti, nn)
                proj(v, 4, mmix, ctiles, sl, ev_m)
            S.barrier()
            ar_off[0] = mark2
            layer_norm(1, ctiles, NT)
            ffn(f2i_d, f2o_d, ctiles, NT)
            layer_norm(2, ctiles, NT)
            mark5 = ar_off[0]
            sgp = [carve(f"sgp{i}", [128, NTM]) for i in range(2)]
            tp = [carve(f"tp{i}", [128, NTM]) for i in range(2)]
            BLK5 = {}
            spp = next_slot(); vpp = spp.ap[:, 0:2048].rearrange("p (kc n) -> p kc n", kc=2)
            wload(spp, None, w_pp_d.rearrange("(kc p) n -> p kc n", p=128), view=lambda a: a[:, 0:2048].rearrange("p (kc n) -> p kc n", kc=2))
            for g2 in range(2):
                sl = next_slot(); v = sl.ap.rearrange("p (kc n) -> p kc n", kc=8)
                wload(sl, None, wview(w_pg_d, g2 * 512, 512), view=V8)
                for ci in range(4):
                    dc = g2 * 4 + ci
                    for ti5, (n0, n) in enumerate(ctiles):
                        k_ = dc % 2
                        if ci % 2 == 0:
                            blk = {}
                            for c2 in range(2):
                                blk[("p", ci + c2)] = nb()
                                for kc in range(2):
                                    d2 = g2 * 4 + ci + c2
                                    mm(blk[("p", ci + c2)].ap[:, 0:n], vpp[:, kc, d2 * 128:(d2 + 1) * 128], pT[kc].ap[:, n0:n0 + n], kc == 0, kc == 1, [spp, pT[kc]], blk[("p", ci + c2)])
                            for c2 in range(2):
                                blk[("g", ci + c2)] = nb()
                            for kc in range(8):
                                for c2 in range(2):
                                    mm(blk[("g", ci + c2)].ap[:, 0:n], v[:, kc, (ci + c2) * 128:(ci + c2 + 1) * 128], xb[kc].ap[:, n0:n0 + n], kc == 0, kc == 7, [sl, xb[kc]], blk[("g", ci + c2)])
                            BLK5[ti5] = blk
                        bg = BLK5[ti5][("g", ci)]; bp = BLK5[ti5][("p", ci)]
                        act(sgp[k_].ap[:, 0:n], bg.ap[:, 0:n], AF.Sigmoid, [bg], [sgp[k_]])
                        tt(tp[k_].ap[:, 0:n], sgp[k_].ap[:, 0:n], bp.ap[:, 0:n], ALU.mult, [sgp[k_], bp], [tp[k_]])
                        stt(xf[dc].ap[:, n0:n0 + n], tp[k_].ap[:, 0:n], IA, xf[dc].ap[:, n0:n0 + n], ALU.mult, ALU.add, [tp[k_], xf[dc]], [xf[dc]])
                        ln_acc(dc, ti5, (n0, n))
            ln_begin(ctiles)
            arena_reset(mark5)
            layer_norm(3, ctiles, NT)
        for tt_i in range(4):
            xo = xin[tt_i % 2]
            for half in range(2):
                bk = nb()
                for f4 in range(4):
                    fc = half * 4 + f4
                    tr(bk.ap[:, f4 * 128:(f4 + 1) * 128], xf[fc].ap[:, tt_i * 128:(tt_i + 1) * 128], ident, [xf[fc], cst], bk, inc=(f4 == 3))
                cp("act" if half == 0 else "dve", xo.ap[:, half * 512:(half + 1) * 512], bk.ap, [bk], [xo])
            yb_ = dbuf("yd")
            S.dma("sp", y_d[t0 + tt_i * 128:t0 + (tt_i + 1) * 128, :], xo.ap, reads=[xo], writes=[yb_])
            out_bufs.append(yb_)
        if use_s:
            xo = xin[0]
            for half in range(2):
                bk = nb()
                for f4 in range(4):
                    fc = half * 4 + f4
                    tr(bk.ap[0:NS, f4 * 128:(f4 + 1) * 128], xf[fc].ap[:, PW:PW + NS], ident, [xf[fc], cst], bk, inc=(f4 == 3))
                cp("act" if half == 0 else "dve", xo.ap[0:NS, half * 512:(half + 1) * 512], bk.ap[0:NS, :], [bk], [xo])
            yb_ = dbuf("ysd")
            S.dma("sp", ys_d, xo.ap[0:NS, :], reads=[xo], writes=[yb_])
            out_bufs.append(yb_)

    if stage < 3:
        S.finish(out_bufs, "sp")
        return nc, S
    ob = dbuf("Cpd")
    S.dma("sp", Cp_d.rearrange("h (dc p) e -> p h dc e", p=128), Cst.ap[:, :, :, 0:256], reads=[Cst], writes=[ob]); out_bufs.append(ob)
    nrow = Buf("nrow", sb("nrow", [8, 128]))
    ncol_ = Buf("ncol", sb("ncol", [128, 8]))
    cp("dve", ncol_.ap, Cst.ap[:, :, :, 256].rearrange("p a b -> p (a b)"), [Cst], [ncol_])
    bk = nb()
    tr(bk.ap[0:8, 0:128], ncol_.ap, ident, [ncol_, cst], bk)
    cp("dve", nrow.ap, bk.ap[0:8, 0:128], [bk], [nrow])
    ob = dbuf("npd"); S.dma("sp", np_d, nrow.ap, reads=[nrow], writes=[ob]); out_bufs.append(ob)
    mfin = Buf("mfin", sb("mfin", [4, 1]))
    tt(mfin.ap, Bc.ap[:, 0:1], Gt.ap[:, 0:1], ALU.add, [Bc, Gt], [mfin])
    with nc.allow_non_contiguous_dma(reason="tiny"):
        ob = dbuf("mpd"); S.dma("sp", mp_d, mfin.ap, reads=[mfin], writes=[ob]); out_bufs.append(ob)
    crow = Buf("crow", sb("crow", [2, D]))
    for half in range(2):
        bk = nb()
        for f4 in range(4):
            fc = half * 4 + f4
            tr(bk.ap[0:2, f4 * 128:(f4 + 1) * 128], ccar.ap[:, fc, :], ident, [ccar, cst], bk, inc=(f4 == 3))
        cp("dve", crow.ap[:, half * 512:(half + 1) * 512], bk.ap[0:2, :], [bk], [crow])
    ob = dbuf("cvpd"); S.dma("sp", cvp_d, crow.ap, reads=[crow], writes=[ob]); out_bufs.append(ob)
    if with_samples:
        prow = xin[1]; prow_ap = xin[1].ap[0:NS, :]
        for half in range(2):
            bk = nb()
            for f4 in range(4):
                fc = half * 4 + f4
                tr(bk.ap[0:NS, f4 * 128:(f4 + 1) * 128], pres_all.ap[:, fc, :], ident, [pres_all, cst], bk, inc=(f4 == 3))
            cp("dve", prow_ap[:, half * 512:(half + 1) * 512], bk.ap[0:NS, :], [bk], [prow])
        ob = dbuf("cvs1"); S.dma("sp", cvs_d[:, 1, :], prow_ap, reads=[prow], writes=[ob]); out_bufs.append(ob)
    S.finish(out_bufs, "sp")
    return nc, S


EXTRA_OUT = []
AR_HW = [0]


def build_samples(L):
    raise NotImplementedError


def conv_samples(L):
    raise NotImplementedError


_CACHE = {}


def _consts():
    c = np.zeros((128, 768), np.float32)
    c[:, 0:128] = np.eye(128, dtype=np.float32)
    s = np.arange(128)[:, None]
    t = np.arange(128)[None, :]
    m = np.where(t >= s, 0.0, -1e30).astype(np.float32)
    c[:, 128:640] = np.tile(m, (1, 4))
    c[0:4, 640:644] = np.eye(4, dtype=np.float32)
    return c


def kernel(x_prompt, x_sample, p_prompt, p_sample, state_C, state_n, state_m, state_conv,
           w_in, b_in, m_norm_g, w_a, w_b, conv_w, conv_b, w_mix,
           ffn1_wi, ffn1_wo, ffn2_wi, ffn2_wo, w_pg, w_pp, ln_g, ln_b):
    f = lambda a: np.ascontiguousarray(np.asarray(a, dtype=np.float32))
    if "nc" not in _CACHE:
        _CACHE["nc"] = build_program(with_samples=WITH_SAMPLES)[0]
    nc = _CACHE["nc"]
    shared = {
        "w_in": f(w_in[0]), "b_in": f(b_in[0]).reshape(1, NIN), "m_norm_g": f(m_norm_g[0]).reshape(8, 128),
        "w_a": f(w_a[0]), "w_b": f(w_b[0]), "conv_w": f(conv_w[0]).reshape(24, 128), "conv_b": f(conv_b[0]).reshape(8, 128),
        "w_mix": f(w_mix[0]), "ffn1_wi": f(ffn1_wi[0]), "ffn1_wo": f(ffn1_wo[0]), "ffn2_wi": f(ffn2_wi[0]),
        "ffn2_wo": f(ffn2_wo[0]), "w_pg": f(w_pg[0]), "w_pp": f(w_pp[0]), "ln_g": f(ln_g[0]).reshape(32, 128),
        "ln_b": f(ln_b[0]).reshape(32, 128), "consts": _consts(),
    }
    in_maps = []
    for b in range(8):
        sl = slice(NS * b, NS * (b + 1))
        m = dict(shared)
        m.update({
            "x": f(x_prompt[b]), "xs": f(x_sample[sl, 0]), "p": f(p_prompt[0, b]), "psm": f(p_sample[0, sl, 0]),
            "sC": f(state_C[0, sl]), "sn": f(state_n[0, sl]).reshape(NS * 8, 128), "sm": f(state_m[0, sl]),
            "scv": f(state_conv[0, sl]).reshape(NS * 2, D),
        })
        in_maps.append(m)
    res = run_bass_kernel_spmd(nc, in_maps, core_ids=list(range(8)))
    R = res.results
    y = np.stack([R[b]["y"] for b in range(8)])
    ys = np.concatenate([R[b]["ys"] for b in range(8)])[:, None, :]
    Cp = np.stack([R[b]["Cp"] for b in range(8)])[None]
    npr = np.stack([R[b]["np"].reshape(4, 256) for b in range(8)])[None]
    mp = np.stack([R[b]["mp"].reshape(4) for b in range(8)])[None]
    cvp = np.stack([R[b]["cvp"] for b in range(8)])[None]
    Cs = np.concatenate([R[b]["Cs"] for b in range(8)])[None]
    ns = np.concatenate([R[b]["ns"].reshape(NS, 4, 256) for b in range(8)])[None]
    ms = np.concatenate([R[b]["ms"] for b in range(8)])[None]
    cvs = np.concatenate([R[b]["cvs"] for b in range(8)])[None]
    return (y, ys, Cp, npr, mp, cvp, Cs, ns, ms, cvs)


WITH_SAMPLES = True
```

```python
import numpy as np
import concourse.bass as bass
import concourse.mybir as mybir
from concourse.bass_utils import run_bass_kernel_spmd

F32 = mybir.dt.float32
F32R = mybir.dt.float32r
BF16 = mybir.dt.bfloat16
AF = mybir.ActivationFunctionType
ALU = mybir.AluOpType

D = 1024
SEQ = 2048
NS = 16
PD = 256
DFF = 2816
NJ = 22
NIN = 9224
Q0, K0, V0, O0, I0, F0, CB0, CC0, CH0, GA0, GB0 = 0, 1024, 2048, 3072, 4096, 4100, 4104, 5128, 6152, 7176, 8200
ALPHA = 2.0 ** 0.25
IA = 1.0 / ALPHA
LN_EPS = 1e-5
EPS_EFF = LN_EPS / (ALPHA * ALPHA)
PW = 512
NPASS = 4
EPOCH = 12000
import os
FORCE_INC = bool(int(os.environ.get('FORCE_INC', '0')))
DBG = os.environ.get('DBG', '')
USE_SCR = False
USE_WARM = True
WK_LN, WK_ML, WK_IN = 22, 20, 24
LN_XR_ENG = 'act'


class Buf:
    __slots__ = ("name", "ap", "w", "r", "dsem", "dcnt", "excl", "arena")

    def __init__(self, name, ap=None, excl=False, arena=False):
        self.arena = arena
        self.name = name
        self.ap = ap
        self.excl = excl
        self.w = []
        self.r = []
        self.dsem = None
        self.dcnt = 0


class Sched:
    def __init__(self, nc):
        self.nc = nc
        self.eng = {"pe": nc.tensor, "act": nc.scalar, "dve": nc.vector, "pool": nc.gpsimd, "sp": nc.sync}
        self.sem = {}
        self.cnt = {}
        self.known = {e: {} for e in self.eng}
        self.nsem = 0
        self.ninst = {e: 0 for e in self.eng}
        self.nwait = {e: 0 for e in self.eng}
        self.arena_toks = []
        self.last_barrier = []
        for e in self.eng:
            self._new_epoch(e)

    def _alloc_sem(self, name):
        self.nsem += 1
        return self.nc.alloc_semaphore(f"{name}_{self.nsem}")

    def _new_epoch(self, e):
        self.sem[e] = self._alloc_sem(f"s_{e}")
        self.cnt[e] = 0

    def _wait(self, e, deps):
        best = {}
        for (s, v) in deps:
            k = id(s)
            if k not in best or best[k][1] < v:
                best[k] = (s, v)
        kn = self.known[e]
        for k, (s, v) in best.items():
            if kn.get(k, 0) >= v:
                continue
            self.eng[e].wait_ge(s, v)
            self.nwait[e] += 1
            kn[k] = v

    def op(self, e, fn, reads=(), writes=(), inc=True, skip_own=False):
        deps = []
        own = self.sem[e]
        for b in reads:
            deps.extend(b.w)
            if b.excl:
                deps.extend(t for t in b.r if t[0] is not own)
        for b in writes:
            deps.extend(b.w)
            deps.extend(b.r)
        if skip_own:
            deps = [d for d in deps if d[0] is not own]
        self._wait(e, deps)
        inst = fn(self.eng[e])
        self.ninst[e] += 1
        if FORCE_INC:
            inc = True
        if inc:
            self.cnt[e] += 1
            inst.then_inc(own, 1)
            tok = (own, self.cnt[e])
        else:
            tok = (own, self.cnt[e] + 1)
        for b in reads:
            b.r.append(tok)
        for b in writes:
            b.w = [tok]
            b.r = []
        if inc and self.cnt[e] >= EPOCH:
            self._new_epoch(e)
        return tok

    def dma(self, q, out_ap, in_ap, reads=(), writes=(), join=False, owner=None, **kw):
        deps = []
        for b in reads:
            deps.extend(b.w)
        for b in writes:
            if not join:
                deps.extend(b.w)
            deps.extend(b.r)
        self._wait(q, deps)
        if owner is None:
            owner = writes[0] if writes else reads[0]
        if owner.dsem is None:
            owner.dsem = self._alloc_sem("d")
        inst = self.eng[q].dma_start(out=out_ap, in_=in_ap, **kw)
        owner.dcnt += 16
        inst.then_inc(owner.dsem, 16)
        tok = (owner.dsem, owner.dcnt)
        self.ninst[q] += 1
        if any(b.arena for b in reads) or any(b.arena for b in writes):
            self.arena_toks.append(tok)
        for b in reads:
            b.r.append(tok)
        for b in writes:
            if join:
                b.w = [t for t in b.w if t[0] is not owner.dsem] + [tok]
            else:
                b.w = [tok]
            b.r = []
        return tok

    def barrier(self, engines=("pe", "act", "dve")):
        toks = [(self.sem[e], self.cnt[e]) for e in engines if self.cnt[e] > 0] + self.arena_toks
        self.arena_toks = []
        for e in engines:
            self._wait(e, toks)
        self.last_barrier = [(self.sem[e], self.cnt[e]) for e in engines if self.cnt[e] > 0]

    def finish(self, bufs, e="sp"):
        deps = []
        for b in bufs:
            deps.extend(b.w)
            deps.extend(b.r)
        self._wait(e, deps)


def build_program(with_samples=True, stage=99, npass=NPASS):
    nc = bass.Bass("TRN2", target_bir_lowering=False)
    S = Sched(nc)

    def din(name, shape):
        return nc.dram_tensor(name, list(shape), F32, kind="ExternalInput").ap()

    def dout(name, shape):
        return nc.dram_tensor(name, list(shape), F32, kind="ExternalOutput").ap()

    x_d = din("x", [SEQ, D]); xs_d = din("xs", [NS, D])
    p_d = din("p", [SEQ, PD]); psm_d = din("psm", [NS, PD])
    sC_d = din("sC", [NS, 4, 256, 256]); sn_d = din("sn", [NS * 4 * 2, 128]); sm_d = din("sm", [NS, 4])
    scv_d = din("scv", [NS * 2, D])
    w_in_d = din("w_in", [D, NIN]); b_in_d = din("b_in", [1, NIN])
    mng_d = din("m_norm_g", [8, 128])
    w_a_d = din("w_a", [D, D]); w_b_d = din("w_b", [D, D])
    cw_d = din("conv_w", [24, 128]); cb_d = din("conv_b", [8, 128])
    w_mix_d = din("w_mix", [D, D])
    f1i_d = din("ffn1_wi", [D, 2 * DFF]); f1o_d = din("ffn1_wo", [DFF, D])
    f2i_d = din("ffn2_wi", [D, 2 * DFF]); f2o_d = din("ffn2_wo", [DFF, D])
    w_pg_d = din("w_pg", [D, D]); w_pp_d = din("w_pp", [PD, D])
    lng_d = din("ln_g", [32, 128]); lnb_d = din("ln_b", [32, 128])
    cst_d = din("consts", [128, 768])

    y_d = dout("y", [SEQ, D]); ys_d = dout("ys", [NS, D])
    Cp_d = dout("Cp", [4, 256, 256]); np_d = dout("np", [8, 128]); mp_d = dout("mp", [4, 1])
    cvp_d = dout("cvp", [2, D])
    Cs_d = dout("Cs", [NS, 4, 256, 256]); ns_d = dout("ns", [NS, D]); ms_d = dout("ms", [NS, 4])
    cvs_d = dout("cvs", [NS, 2, D])
    out_bufs = []

    def dbuf(name):
        b = Buf(name)
        return b

    wsrc = Buf("wsrc")
    def sb(name, shape, dt=F32):
        return nc.alloc_sbuf_tensor(name, list(shape), dt).ap()

    NTM = PW + NS
    cst = Buf("cst", sb("cst", [128, 768]))
    ident = cst.ap[:, 0:128]
    maskadd4 = cst.ap[:, 128:640]
    eye4 = cst.ap[0:4, 640:644]
    xf_t = sb("xf", [128, 8, NTM]); xb_t = sb("xb", [128, 8, NTM], BF16)
    xf = [Buf(f"xf{i}", xf_t[:, i, :]) for i in range(8)]
    xb = [Buf(f"xb{i}", xb_t[:, i, :]) for i in range(8)]
    xin = [Buf(f"xin{i}", sb(f"xin{i}", [128, D])) for i in range(2)]
    pin = [Buf(f"pin{i}", sb(f"pin{i}", [128, PD])) for i in range(2)]
    pT_t = sb("pT", [128, 2, NTM], BF16)
    pT = [Buf(f"pT{i}", pT_t[:, i, :]) for i in range(2)]
    ring = [Buf(f"ring{i}", sb(f"ring{i}", [128, 4096], BF16)) for i in range(4)]
    ring_i = [0]

    NSL = 96
    wscr = nc.dram_tensor("wscr", [NSL, 128, 4096], BF16).ap()
    scr = [Buf(f"scr{i}") for i in range(NSL)]
    slot_idx = [0]
    cur_idx = {}
    pass_no = [0]

    def next_slot():
        s = ring[ring_i[0] % 4]
        ring_i[0] += 1
        cur_idx[s.name] = slot_idx[0]
        slot_idx[0] += 1
        assert slot_idx[0] <= NSL
        return s

    Cst = Buf("Cst", sb("Cst", [128, 4, 2, 257]))
    Cb0_ = Buf("Cb0", sb("Cb0", [128, 4, 2, 258], BF16)); Cb = [Cb0_, Cb0_]
    cols = Buf("cols", sb("cols", [128, 192]))
    bcol = cols.ap[:, 0:72]; cwc = cols.ap[:, 72:96]; cbc = cols.ap[:, 96:104]; mngc = cols.ap[:, 104:112]
    lngc = cols.ap[:, 112:144]; lnbc = cols.ap[:, 144:176]; bk16 = cols.ap[:, 176:184]
    rows = Buf("rows", sb("rows", [128, 128]))
    smallc = Buf("smallc", sb("smallc", [128, 8]))
    epsc = smallc.ap[:, 0:1]; onec = smallc.ap[:, 1:2]; bic = smallc.ap[0:4, 2:3]; bfc = smallc.ap[0:4, 3:4]
    eps5c = smallc.ap[:, 4:5]
    ones_r = Buf("ones_r", sb("ones_r", [128, 128], F32R))
    ones4 = Buf("ones4", sb("ones4", [4, 128]))
    wg = Buf("wg", sb("wg", [128, 8, 8], BF16))
    Bc = Buf("Bc", sb("Bc", [4, 513])); Gt = Buf("Gt", sb("Gt", [4, 513]))
    ccar = Buf("ccar", sb("ccar", [128, 8, 2]))
    colf = Buf("colf", sb("colf", [128, 4, 12]))
    Gb = Buf("Gb", sb("Gb", [128, 5, 4]))
    sm4 = Buf("sm4", sb("sm4", [128, 4, 4, 4]))
    wcb = sm4.ap[:, 0]; ds16 = sm4.ap[:, 1]; winter = sm4.ap[:, 2]; nrm = sm4.ap[:, 3]
    st8 = Buf("st8", sb("st8", [128, 64]))
    sq_p = [Buf(f"sq{i}", sb(f"sq{i}", [128, NTM], F32R)) for i in range(2)]
    xr_p = [Buf(f"xr{i}", sb(f"xr{i}", [128, NTM], F32R)) for i in range(2)]
    pres_all = Buf("pres_all", sb("pres_all", [128, 8, NS]))
    cbufT = Buf("cbufT", sb("cbufT", [128, 8, 2 * NS]))
    ASZ = 102400
    arena_t = sb("arena", [128, ASZ // 4])
    ar_off = [0]

    def carve(name, shape, dt=F32):
        n = int(np.prod(shape[1:]))
        nb = n * (4 if dt in (F32, F32R) else 2)
        nb4 = (nb + 3) // 4
        o = ar_off[0]
        assert (o + nb4) * 4 <= ASZ, (name, o * 4, nb)
        ar_off[0] = o + nb4 + (-(nb4) % 8)
        AR_HW[0] = max(AR_HW[0], ar_off[0] * 4)
        ap = arena_t[0:shape[0], o:o + nb4]
        if dt != F32:
            ap = ap.bitcast(dt)
        if dt == BF16:
            ap = ap[:, 0:n]
        if len(shape) == 3:
            ap = ap.rearrange("p (a b) -> p a b", a=shape[1])
        elif len(shape) == 4:
            ap = ap.rearrange("p (a b c) -> p a b c", a=shape[1], b=shape[2])
        b_ = Buf(name, ap, arena=True)
        b_.w = list(S.last_barrier)
        return b_

    def arena_reset(mark=0):
        S.barrier()
        ar_off[0] = mark

    banks = [Buf(f"bank{i}", nc.alloc_psum_tensor(f"bank{i}", [128, 512], F32).ap(), excl=True) for i in range(8)]
    bank_i = [0]

    reserved = set()

    def nb():
        while True:
            b = banks[bank_i[0] % 8]
            bank_i[0] += 1
            if b.name not in reserved:
                return b

    def mm(out_ap, lhsT, rhs, start, stop, reads, wbuf, inc=None):
        if inc is None:
            inc = stop
        S.op("pe", lambda e: e.matmul(out_ap, lhsT=lhsT, rhs=rhs, start=start, stop=stop),
             reads=reads, writes=[wbuf], inc=inc, skip_own=True)

    def tr(out_ap, in_ap, idn, reads, wbuf, inc=True):
        S.op("pe", lambda e: e.transpose(out_ap, in_ap, idn), reads=reads, writes=[wbuf], inc=inc, skip_own=True)

    dmy_w = Buf("dmy_w", sb("dmy_w", [128, 128], BF16)); dmy_x = Buf("dmy_x", sb("dmy_x", [128, 512], BF16))

    def warm(K):
        if K <= 0 or not USE_WARM:
            return
        bk = nb()
        for _ in range(K):
            S.op("pe", lambda e: e.matmul(bk.ap, lhsT=dmy_w.ap, rhs=dmy_x.ap, start=True, stop=True),
                 reads=[dmy_w, dmy_x], writes=[bk], inc=False, skip_own=True)

    def act(out, in_, func, reads, writes, bias=None, scale=None):
        kw = {}
        if bias is not None:
            kw["bias"] = bias
        if scale is not None:
            kw["scale"] = scale
        S.op("act", lambda e: e.activation(out=out, in_=in_, func=func, **kw), reads=reads, writes=writes)

    def tt(out, in0, in1, op, reads, writes, eng="dve"):
        S.op(eng, lambda e: e.tensor_tensor(out=out, in0=in0, in1=in1, op=op), reads=reads, writes=writes)

    def stt(out, in0, scalar, in1, op0, op1, reads, writes):
        S.op("dve", lambda e: e.scalar_tensor_tensor(out=out, in0=in0, scalar=scalar, in1=in1, op0=op0, op1=op1),
             reads=reads, writes=writes)

    def ts(out, in0, s1, s2, op0, op1, reads, writes, eng="dve"):
        S.op(eng, lambda e: e.tensor_scalar(out=out, in0=in0, scalar1=s1, scalar2=s2, op0=op0, op1=op1),
             reads=reads, writes=writes)

    def cp(eng, out, in_, reads, writes):
        if eng == "act":
            act(out, in_, AF.Copy, reads, writes)
        else:
            S.op(eng, lambda e: e.tensor_copy(out=out, in_=in_), reads=reads, writes=writes)

    def wload(dst_buf, dst_ap, src_ap, join=False, view=None):
        if view is None or not USE_SCR:
            if view is not None:
                dst_ap = view(dst_buf.ap)
            S.dma("pool", dst_ap, src_ap, reads=[wsrc], writes=[dst_buf], join=join)
            return
        i = cur_idx[dst_buf.name]
        if pass_no[0] == 0:
            S.dma("pool", view(dst_buf.ap), src_ap, reads=[wsrc], writes=[dst_buf], join=join)
            S.dma("sp", view(wscr[i]), view(dst_buf.ap), reads=[dst_buf], writes=[scr[i]], join=True, owner=dst_buf)
        else:
            S.dma("pool", view(dst_buf.ap), view(wscr[i]), reads=[scr[i]], writes=[dst_buf], join=join)

    V8 = lambda a: a.rearrange("p (kc n) -> p kc n", kc=8)

    def wview(dram, c0, ncol):
        return dram.rearrange("(kc p) n -> p kc n", p=128)[:, :, c0:c0 + ncol]

    S.dma("sp", cst.ap, cst_d, reads=[wsrc], writes=[cst])
    S.dma("sp", rows.ap[0:32, :], b_in_d[0, 0:4096].rearrange("(r c) -> r c", c=128), reads=[wsrc], writes=[rows])
    S.dma("sp", rows.ap[32:72, :], b_in_d[0, CB0:NIN].rearrange("(r c) -> r c", c=128), reads=[wsrc], writes=[rows], join=True)
    S.dma("sp", rows.ap[72:96, :], cw_d, reads=[wsrc], writes=[rows], join=True)
    S.dma("sp", rows.ap[96:104, :], cb_d, reads=[wsrc], writes=[rows], join=True)
    S.dma("sp", rows.ap[104:112, :], mng_d, reads=[wsrc], writes=[rows], join=True)
    b0 = nb()
    tr(b0.ap[:, 0:112], rows.ap[0:112, :], ident[0:112, 0:112], [rows, cst], b0)
    cp("dve", cols.ap[:, 0:112], b0.ap[:, 0:112], [b0], [cols])
    rows2 = Buf("rows2", sb("rows2", [64, 128]))
    S.dma("sp", rows2.ap[0:32, :], lng_d, reads=[wsrc], writes=[rows2])
    S.dma("sp", rows2.ap[32:64, :], lnb_d, reads=[wsrc], writes=[rows2], join=True)
    b0 = nb()
    tr(b0.ap[:, 0:64], rows2.ap[0:64, :], ident[0:64, 0:64], [rows2, cst], b0)
    cp("dve", cols.ap[:, 112:176], b0.ap[:, 0:64], [b0], [cols])
    S.op("dve", lambda e: e.tensor_scalar(out=bk16, in0=bcol[:, 8:16], scalar1=1.0 / 16.0, scalar2=None, op0=ALU.mult),
         reads=[cols], writes=[cols])
    S.op("dve", lambda e: e.memset(smallc.ap[:, 0:1], EPS_EFF), writes=[smallc])
    S.op("dve", lambda e: e.memset(smallc.ap[:, 1:2], 1.0), reads=[], writes=[smallc])
    S.op("dve", lambda e: e.memset(smallc.ap[:, 4:5], LN_EPS), reads=[], writes=[smallc])
    with nc.allow_non_contiguous_dma(reason="tiny gate-bias columns"):
        S.dma("sp", bic, b_in_d[0, I0:I0 + 4].rearrange("(p o) -> p o", o=1), reads=[wsrc], writes=[smallc], join=True)
        S.dma("sp", bfc, b_in_d[0, F0:F0 + 4].rearrange("(p o) -> p o", o=1), reads=[wsrc], writes=[smallc], join=True)
        wload(wg, wg.ap, wview(w_in_d, I0, 8))
    tmp1 = Buf("tmp1", sb("tmp1", [128, 128]))
    S.op("dve", lambda e: e.memset(tmp1.ap, 1.0), writes=[tmp1])
    cp("dve", ones_r.ap, tmp1.ap, [tmp1], [ones_r])
    cp("dve", ones4.ap, tmp1.ap[0:4, :], [tmp1], [ones4])
    S.op("dve", lambda e: e.memset(Cst.ap, 0.0), writes=[Cst])
    S.op("dve", lambda e: e.memset(dmy_w.ap, 0.0), writes=[dmy_w])
    S.op("dve", lambda e: e.memset(dmy_x.ap, 0.0), writes=[dmy_x])
    S.op("dve", lambda e: e.memset(Cb[0].ap, 0.0), writes=[Cb[0]])
    S.op("dve", lambda e: e.memset(Bc.ap[:, 0:1], 0.0), writes=[Bc])
    S.op("dve", lambda e: e.memset(Gt.ap[:, 0:1], 0.0), writes=[Gt])
    S.op("dve", lambda e: e.memset(ccar.ap, 0.0), writes=[ccar])

    if stage == -1:
        ob = Buf("dbg")
        S.dma("sp", y_d[0:128, 0:192], cols.ap, reads=[cols, smallc, wg, ones_r, ones4, Cst, Bc, Gt, ccar], writes=[ob])
        S.finish([ob], "sp")
        return nc, S
    def proj(slot_ap, ncol_chunks, rhs_list, ctiles, slot_buf, evac, kcn=8, col0=0):
        for ti, (n0, n) in enumerate(ctiles):
            bks = [nb() for _ in range(ncol_chunks)]
            for kc in range(kcn):
                for ci in range(ncol_chunks):
                    mm(bks[ci].ap[:, 0:n], slot_ap[:, kc, col0 + ci * 128: col0 + (ci + 1) * 128], rhs_list[kc].ap[:, n0:n0 + n],
                       kc == 0, kc == kcn - 1, [slot_buf, rhs_list[kc]], bks[ci])
            for ci in range(ncol_chunks):
                evac(ci, ti, (n0, n), bks[ci])

    def ln_begin(ctiles):
        act(smallc.ap[:, 5:6], smallc.ap[:, 1:2], AF.Sqrt, [smallc], [smallc])

    def ln_acc(fc, ti, nn):
        pass

    LN_BASE = (ASZ - 5 * NTM * 4) // 4
    _o = ar_off[0]
    ar_off[0] = LN_BASE
    ln_mean = carve("mean", [128, NTM]); ln_var = carve("var", [128, NTM]); ln_rstd = carve("rstd", [128, NTM])
    ln_u = [carve(f"u{i}", [128, NTM]) for i in range(2)]
    ar_off[0] = _o
    AR_HW[0] = 0

    def layer_norm(ln, ctiles, NT):
        sq = sq_p; xr = xr_p
        mean, var, rstd, u = ln_mean, ln_var, ln_rstd, ln_u
        for ti, (n0, n) in enumerate(ctiles):
            s1 = nb(); s2 = nb()
            for fc in range(8):
                q_ = sq[fc % 2]; r_ = xr[fc % 2]
                tt(q_.ap[:, 0:n], xf[fc].ap[:, n0:n0 + n], xf[fc].ap[:, n0:n0 + n], ALU.mult, [xf[fc]], [q_])
                cp(LN_XR_ENG, r_.ap[:, 0:n], xf[fc].ap[:, n0:n0 + n], [xf[fc]], [r_])
                mm(s1.ap[:, 0:n], ones_r.ap, r_.ap[:, 0:n], fc == 0, fc == 7, [ones_r, r_], s1, inc=True)
                mm(s2.ap[:, 0:n], ones_r.ap, q_.ap[:, 0:n], fc == 0, fc == 7, [ones_r, q_], s2, inc=True)
            sl = slice(n0, n0 + n)
            warm(WK_LN if n >= 256 else 4)
            act(rstd.ap[:, sl], s1.ap[:, 0:n], AF.Square, [s1], [rstd], scale=1.0 / D)
            stt(var.ap[:, sl], s2.ap[:, 0:n], 1.0 / D, rstd.ap[:, sl], ALU.mult, ALU.subtract, [s2, rstd], [var])
            act(mean.ap[:, sl], s1.ap[:, 0:n], AF.Copy, [s1], [mean], scale=1.0 / D)
            act(var.ap[:, sl], var.ap[:, sl], AF.Sqrt, [var], [var], bias=epsc)
            for fc in range(2):
                tt(u[fc].ap[:, 0:n], xf[fc].ap[:, sl], mean.ap[:, sl], ALU.subtract, [xf[fc], mean], [u[fc]])
            S.op("dve", lambda e: e.reciprocal(out=rstd.ap[:, sl], in_=var.ap[:, sl]), reads=[var], writes=[rstd])
            for fc in range(8):
                u_ = u[fc % 2]
                if fc >= 2:
                    tt(u_.ap[:, 0:n], xf[fc].ap[:, sl], mean.ap[:, sl], ALU.subtract, [xf[fc], mean], [u_])
                tt(u_.ap[:, 0:n], u_.ap[:, 0:n], rstd.ap[:, sl], ALU.mult, [u_, rstd], [u_])
                gcol = lngc[:, ln * 8 + fc: ln * 8 + fc + 1]; bcl = lnbc[:, ln * 8 + fc: ln * 8 + fc + 1]
                act(xb[fc].ap[:, sl], u_.ap[:, 0:n], AF.Identity, [u_, cols], [xb[fc]], bias=bcl, scale=gcol)
                act(xf[fc].ap[:, sl], u_.ap[:, 0:n], AF.Identity, [u_, cols], [xf[fc]], bias=bcl, scale=gcol)

    def ffn(wi_d, wo_d, ctiles, NT):
        mark = ar_off[0]
        actb = [carve(f"act{j}", [128, NTM], BF16) for j in range(NJ)]
        sg = [carve(f"sg{i}", [128, NTM]) for i in range(2)]
        k = 0
        for jp in range(NJ // 2):
            sl = next_slot()
            v = sl.ap.rearrange("p (kc n) -> p kc n", kc=8)
            wload(sl, None, wview(wi_d, jp * 256, 256), view=lambda a: a.rearrange("p (kc n) -> p kc n", kc=8)[:, :, 0:256])
            wload(sl, None, wview(wi_d, DFF + jp * 256, 256), join=True, view=lambda a: a.rearrange("p (kc n) -> p kc n", kc=8)[:, :, 256:512])
            for (n0, n) in ctiles:
                bks = [nb() for _ in range(4)]
                offs = [0, 256, 128, 384]
                for kc in range(8):
                    for q4 in range(4):
                        mm(bks[q4].ap[:, 0:n], v[:, kc, offs[q4]:offs[q4] + 128], xb[kc].ap[:, n0:n0 + n], kc == 0, kc == 7, [sl, xb[kc]], bks[q4])
                for jj in range(2):
                    j = jp * 2 + jj
                    gb_ = bks[2 * jj]; ub_ = bks[2 * jj + 1]
                    s_ = sg[k % 2]; k += 1
                    act(s_.ap[:, 0:n], gb_.ap[:, 0:n], AF.Silu, [gb_], [s_])
                    tt(actb[j].ap[:, n0:n0 + n], s_.ap[:, 0:n], ub_.ap[:, 0:n], ALU.mult, [s_, ub_], [actb[j]])
        ln_begin(ctiles)
        for dc in range(8):
            sl = next_slot()
            v = sl.ap[:, 0:NJ * 128].rearrange("p (j n) -> p j n", j=NJ)
            wload(sl, None, wo_d.rearrange("(j p) n -> p j n", p=128)[:, :, dc * 128:(dc + 1) * 128], view=lambda a: a[:, 0:NJ * 128].rearrange("p (j n) -> p j n", j=NJ))
            for ti, (n0, n) in enumerate(ctiles):
                bk = nb()
                for j in range(NJ):
                    mm(bk.ap[:, 0:n], v[:, j, :], actb[j].ap[:, n0:n0 + n], j == 0, j == NJ - 1, [sl, actb[j]], bk)
                stt(xf[dc].ap[:, n0:n0 + n], bk.ap[:, 0:n], 0.5 * IA, xf[dc].ap[:, n0:n0 + n], ALU.mult, ALU.add, [bk, xf[dc]], [xf[dc]])
                ln_acc(dc, ti, (n0, n))
        arena_reset(mark)

    for ps_i in range(npass):
        last = (ps_i == NPASS - 1)
        slot_idx[0] = 0
        pass_no[0] = ps_i
        use_s = last and with_samples
        NT = PW + (NS if use_s else 0)
        ctiles = [(0, PW)] + ([(PW, NS)] if use_s else [])
        t0 = ps_i * PW
        if ps_i > 0:
            warm(WK_IN)
        for tt_i in range(4):
            xi = xin[tt_i % 2]; pi = pin[tt_i % 2]
            S.dma("sp", xi.ap, x_d[t0 + tt_i * 128: t0 + (tt_i + 1) * 128, :], reads=[wsrc], writes=[xi])
            S.dma("sp", pi.ap, p_d[t0 + tt_i * 128: t0 + (tt_i + 1) * 128, :], reads=[wsrc], writes=[pi])
            for half in range(2):
                bk = nb()
                for f4 in range(4):
                    fc = half * 4 + f4
                    tr(bk.ap[:, f4 * 128:(f4 + 1) * 128], xi.ap[:, fc * 128:(fc + 1) * 128], ident, [xi, cst], bk, inc=(f4 == 3))
                dst = xf_t[:, half * 4:(half + 1) * 4, tt_i * 128:(tt_i + 1) * 128]
                dstb = xb_t[:, half * 4:(half + 1) * 4, tt_i * 128:(tt_i + 1) * 128]
                src = bk.ap.rearrange("p (a b) -> p a b", a=4)
                grp = xf[half * 4:(half + 1) * 4]; grpb = xb[half * 4:(half + 1) * 4]
                cp("act", dst, src, [bk], grp)
                if 'nobf' not in DBG:
                    cp("dve", dstb, src, [bk], grpb)
            if 'nop' in DBG:
                continue
            bk = nb()
            for kc in range(2):
                tr(bk.ap[:, kc * 128:(kc + 1) * 128], pi.ap[:, kc * 128:(kc + 1) * 128], ident, [pi, cst], bk, inc=(kc == 1))
            cp("dve", pT_t[:, :, tt_i * 128:(tt_i + 1) * 128], bk.ap[:, 0:256].rearrange("p (a b) -> p a b", a=2), [bk], pT)
        if use_s:
            xi = xin[0]; pi = pin[0]
            S.dma("sp", xi.ap[0:NS, :], xs_d, reads=[wsrc], writes=[xi])
            S.dma("sp", pi.ap[0:NS, :], psm_d, reads=[wsrc], writes=[pi])
            for half in range(2):
                bk = nb()
                for f4 in range(4):
                    fc = half * 4 + f4
                    tr(bk.ap[:, f4 * NS:(f4 + 1) * NS], xi.ap[0:NS, fc * 128:(fc + 1) * 128], ident[0:NS, 0:NS], [xi, cst], bk, inc=(f4 == 3))
                src = bk.ap[:, 0:4 * NS].rearrange("p (a b) -> p a b", a=4)
                cp("act", xf_t[:, half * 4:(half + 1) * 4, PW:PW + NS], src, [bk], xf[half * 4:(half + 1) * 4])
                cp("dve", xb_t[:, half * 4:(half + 1) * 4, PW:PW + NS], src, [bk], xb[half * 4:(half + 1) * 4])
            bk = nb()
            for kc in range(2):
                tr(bk.ap[:, kc * NS:(kc + 1) * NS], pi.ap[0:NS, kc * 128:(kc + 1) * 128], ident[0:NS, 0:NS], [pi, cst], bk, inc=(kc == 1))
            cp("dve", pT_t[:, :, PW:PW + NS], bk.ap[:, 0:2 * NS].rearrange("p (a b) -> p a b", a=2), [bk], pT)

        if stage == -2:
            ob = Buf("dbg")
            S.dma("sp", y_d[0:128, 0:512], xf_t[:, 0, 0:512], reads=xf + xb + pT, writes=[ob])
            S.finish([ob], "sp")
            return nc, S
        if stage >= 1:
            ffn(f1i_d, f1o_d, ctiles, NT)
        if stage >= 2:
            layer_norm(0, ctiles, NT)

        if stage >= 3:
            mark2 = ar_off[0]
            qT = [carve(f"qT{i}", [128, NTM], BF16) for i in range(8)]
            kT = [carve(f"kT{i}", [128, NTM], BF16) for i in range(8)]
            kw = carve("kw", [128, 4, 1024], BF16)
            vaug = carve("vaug", [128, 4, 4, 258], BF16)
            sigo = [carve(f"sigo{i}", [128, NTM]) for i in range(8)]
            yain = [carve(f"yain{i}", [128, NTM], BF16) for i in range(8)]
            G1 = carve("G1", [4, NTM]); G2 = carve("G2", [4, NTM]); G3 = carve("G3", [4, NTM]); G4 = carve("G4", [4, NTM])
            mark2b = ar_off[0]
            for (n0, n) in ctiles:
                gi = nb(); gf = nb()
                for kc in range(8):
                    mm(gi.ap[0:4, 0:n], wg.ap[:, kc, 0:4], xb[kc].ap[:, n0:n0 + n], kc == 0, kc == 7, [wg, xb[kc]], gi)
                    mm(gf.ap[0:4, 0:n], wg.ap[:, kc, 4:8], xb[kc].ap[:, n0:n0 + n], kc == 0, kc == 7, [wg, xb[kc]], gf)
                act(G1.ap[:, n0:n0 + n], gi.ap[0:4, 0:n], AF.Identity, [gi, smallc], [G1], bias=bic)
                act(G2.ap[:, n0:n0 + n], gf.ap[0:4, 0:n], AF.Identity, [gf, smallc], [G2], bias=bfc)
            act(G3.ap[:, 0:NT], G2.ap[:, 0:NT], AF.Abs, [G2], [G3])
            act(G3.ap[:, 0:NT], G3.ap[:, 0:NT], AF.Exp, [G3], [G3], scale=-1.0)
            act(G3.ap[:, 0:NT], G3.ap[:, 0:NT], AF.Ln, [G3, smallc], [G3], bias=onec[0:4, :])
            S.op("dve", lambda e: e.tensor_scalar_min(out=G4.ap[:, 0:NT], in0=G2.ap[:, 0:NT], scalar1=0.0), reads=[G2], writes=[G4])
            tt(G2.ap[:, 0:NT], G4.ap[:, 0:NT], G3.ap[:, 0:NT], ALU.subtract, [G4, G3], [G2])
            S.op("dve", lambda e: e.memset(G4.ap[:, 0:PW], 1.0), writes=[G4])
            S.op("dve", lambda e: e.tensor_tensor_scan(out=Bc.ap[:, 1:1 + PW], data0=G4.ap[:, 0:PW], data1=G2.ap[:, 0:PW],
                                                        initial=Bc.ap[:, 0:1], op0=ALU.mult, op1=ALU.add), reads=[G4, G2, Bc], writes=[Bc])
            tt(G3.ap[:, 0:PW], G1.ap[:, 0:PW], Bc.ap[:, 1:1 + PW], ALU.subtract, [G1, Bc], [G3])
            S.op("dve", lambda e: e.tensor_tensor_scan(out=Gt.ap[:, 1:1 + PW], data0=G3.ap[:, 0:PW], data1=G3.ap[:, 0:PW],
                                                        initial=Gt.ap[:, 0:1], op0=ALU.max, op1=ALU.max), reads=[G3, Gt], writes=[Gt])
            tt(G4.ap[:, 0:PW], Bc.ap[:, 1:1 + PW], Gt.ap[:, 1:1 + PW], ALU.add, [Bc, Gt], [G4])
            for g2 in range(2):
                sl = next_slot(); v = sl.ap.rearrange("p (kc n) -> p kc n", kc=8)
                wload(sl, None, wview(w_in_d, Q0 + g2 * 512, 512), view=V8)

                def ev_q(ci, ti, nn, bk, g2=g2):
                    c = g2 * 4 + ci
                    act(qT[c].ap[:, nn[0]:nn[0] + nn[1]], bk.ap[:, 0:nn[1]], AF.Identity, [bk, cols], [qT[c]], bias=bcol[:, c:c + 1])
                proj(v, 4, xb, ctiles, sl, ev_q)
            for c in range(4):
                bk = nb()
                cs = slice(c * 128, (c + 1) * 128)
                tr(bk.ap[:, 0:4], G3.ap[0:4, cs], ident[0:4, 0:4], [G3, cst], bk, inc=False)
                tr(bk.ap[:, 4:8], Gt.ap[0:4, 1 + c * 128: 1 + (c + 1) * 128], ident[0:4, 0:4], [Gt, cst], bk, inc=False)
                tr(bk.ap[:, 8:12], G4.ap[0:4, cs], ident[0:4, 0:4], [G4, cst], bk, inc=True)
                cp("act", colf.ap[:, c, :], bk.ap[:, 0:12], [bk], [colf])
            Bm5 = carve("Bm5", [4, 5, 4])
            tt(Bm5.ap, Gt.ap[0:4, 0:513:128].unsqueeze(2).to_broadcast([4, 5, 4]), eye4.unsqueeze(1).to_broadcast([4, 5, 4]),
               ALU.mult, [Gt, cst], [Bm5])
            bk = nb()
            mm(bk.ap[:, 0:20], ones4.ap, Bm5.ap.rearrange("p a b -> p (a b)"), True, True, [ones4, Bm5], bk)
            cp("act", Gb.ap.rearrange("p a b -> p (a b)"), bk.ap[:, 0:20], [bk], [Gb])
            tt(wcb, Gb.ap[:, 0:4, :], Gb.ap[:, 1:5, :], ALU.subtract, [Gb], [sm4])
            tt(ds16, colf.ap[:, :, 0:4], Gb.ap[:, 1:5, :], ALU.subtract, [colf, Gb], [sm4])
            tt(winter, Gb.ap[:, 0:4, :], colf.ap[:, :, 4:8], ALU.subtract, [Gb, colf], [sm4])
            S.op("dve", lambda e: e.tensor_scalar(out=nrm, in0=colf.ap[:, :, 8:12], scalar1=-1.0, scalar2=None, op0=ALU.mult), reads=[colf], writes=[sm4])
            act(sm4.ap.rearrange("p a b c -> p (a b c)"), sm4.ap.rearrange("p a b c -> p (a b c)"), AF.Exp, [sm4], [sm4])
            S.op("dve", lambda e: e.tensor_scalar(out=ds16, in0=ds16, scalar1=1.0 / 16.0, scalar2=None, op0=ALU.mult), reads=[sm4], writes=[sm4])

            S.op("dve", lambda e: e.memset(vaug.ap[:, :, :, 256:258], 1.0), writes=[vaug])
            bias_hl = carve("bias_hl", [1, 2, 2048], BF16)
            ones_bf = carve("ones_bf", [1, 128], BF16)
            mark2c = ar_off[0]
            bias_f = carve("bias_f", [1, 1024])
            bias_t = carve("bias_t", [1, 1024])
            for hb in range(2):
                hs_ = slice(hb * 1024, (hb + 1) * 1024)
                S.dma("sp", bias_f.ap, b_in_d[0:1, K0 + hb * 1024:K0 + (hb + 1) * 1024], reads=[wsrc], writes=[bias_f])
                cp("dve", bias_hl.ap[:, 0, hs_], bias_f.ap, [bias_f], [bias_hl])
                cp("dve", bias_t.ap, bias_hl.ap[:, 0, hs_], [bias_hl], [bias_t])
                tt(bias_t.ap, bias_f.ap, bias_t.ap, ALU.subtract, [bias_f, bias_t], [bias_t])
                cp("dve", bias_hl.ap[:, 1, hs_], bias_t.ap, [bias_t], [bias_hl])
            S.op("dve", lambda e: e.memset(ones_bf.ap, 1.0), writes=[ones_bf])
            S.last_barrier = list(S.last_barrier) + bias_f.w + bias_f.r + bias_t.w + bias_t.r
            ar_off[0] = mark2c
            vTs = carve("vTs", [128, 8, NS])
            for g2 in range(4):
                sl = next_slot(); v = sl.ap.rearrange("p (kc n) -> p kc n", kc=8)
                wload(sl, None, wview(w_in_d, K0 + g2 * 512, 512), view=V8)
                if g2 < 2:
                    def ev_k(ci, ti, nn, bk, g2=g2):
                        c = g2 * 4 + ci
                        act(kT[c].ap[:, nn[0]:nn[0] + nn[1]], bk.ap[:, 0:nn[1]], AF.Identity, [bk, cols], [kT[c]],
                            bias=bk16[:, c:c + 1], scale=1.0 / 16.0)
                    proj(v, 4, xb, ctiles, sl, ev_k)
                for c in range(4):
                    bk = nb()
                    cs = slice(c * 128, (c + 1) * 128)
                    for kc in range(8):
                        mm(bk.ap, xb[kc].ap[:, cs], v[:, kc, :], kc == 0, False, [sl, xb[kc]], bk, inc=False)
                    mm(bk.ap, ones_bf.ap, bias_hl.ap[:, 0, g2 * 512:(g2 + 1) * 512], False, False, [ones_bf, bias_hl], bk, inc=False)
                    mm(bk.ap, ones_bf.ap, bias_hl.ap[:, 1, g2 * 512:(g2 + 1) * 512], False, True, [ones_bf, bias_hl], bk, inc=True)
                    for hh_ in range(2):
                        h = (g2 % 2) * 2 + hh_
                        if g2 < 2:
                            act(kw.ap[:, c, h * 256:(h + 1) * 256], bk.ap[:, hh_ * 256:(hh_ + 1) * 256], AF.Copy, [bk, sm4], [kw],
                                scale=ds16[:, c, h:h + 1])
                        else:
                            cp("dve", vaug.ap[:, c, h, 0:256], bk.ap[:, hh_ * 256:(hh_ + 1) * 256], [bk], [vaug])
                if use_s and g2 >= 2:
                    for ci in range(4):
                        c = (g2 - 2) * 4 + ci
                        bk = nb()
                        for kc in range(8):
                            mm(bk.ap[:, 0:NS], v[:, kc, ci * 128:(ci + 1) * 128], xb[kc].ap[:, PW:PW + NS], kc == 0, kc == 7, [sl, xb[kc]], bk)
                        act(vTs.ap[:, c, :], bk.ap[:, 0:NS], AF.Identity, [bk, cols], [vTs], bias=bcol[:, 16 + c:17 + c])
            for g2 in range(2):
                sl = next_slot(); v = sl.ap.rearrange("p (kc n) -> p kc n", kc=8)
                wload(sl, None, wview(w_in_d, O0 + g2 * 512, 512), view=V8)

                def ev_o(ci, ti, nn, bk, g2=g2):
                    c = g2 * 4 + ci
                    act(sigo[c].ap[:, nn[0]:nn[0] + nn[1]], bk.ap[:, 0:nn[1]], AF.Sigmoid, [bk, cols], [sigo[c]], bias=bcol[:, 24 + c:25 + c])
                proj(v, 4, xb, ctiles, sl, ev_o)

            mark_ml = ar_off[0]
            Bm2 = carve("Bm2", [4, 4, 128])
            argb = [carve(f"arg{i}", [128, 4, 128]) for i in range(2)]
            Dm = [carve(f"Dm{i}", [128, 512]) for i in range(2)]
            Stb = [carve(f"St{i}", [128, 512], BF16) for i in range(2)]
            numdb = [carve(f"numd{i}", [128, 4, 257]) for i in range(2)]
            st_dn = Buf("st_dn", st8.ap[:, 0:8]); st_rs = Buf("st_rs", st8.ap[:, 8:16]); st_bn = Buf("st_bn", st8.ap[:, 16:48])
            st_dn.w = list(st8.w) + list(st8.r); st_rs.w = list(st_dn.w); st_bn.w = list(st_dn.w)
            dn = st8.ap[:, 0:4]; rden = st8.ap[:, 4:8]; sd = st8.ap[:, 8:12]; rs = st8.ap[:, 12:16]
            bst = st8.ap[:, 16:40].rearrange("p (h s) -> p h s", h=4)
            mv = st8.ap[:, 40:48].rearrange("p (h s) -> p h s", h=4)
            CK = {}

            def csl(c):
                return slice(c * 128, (c + 1) * 128)

            def G_dve1(c):
                tt(Bm2.ap, Gt.ap[0:4, 1 + c * 128:1 + (c + 1) * 128].unsqueeze(1).to_broadcast([4, 4, 128]),
                   eye4.unsqueeze(2).to_broadcast([4, 4, 128]), ALU.mult, [Gt, cst], [Bm2])

            def G_pe(c):
                ng = nb(); CK[("ng", c)] = ng
                mm(ng.ap, ones4.ap, Bm2.ap.rearrange("p a b -> p (a b)"), True, True, [ones4, Bm2], ng)

            def G_dve2(c):
                ng = CK[("ng", c)]; ar_ = argb[c % 2]
                stt(ar_.ap.rearrange("p a b -> p (a b)"), ng.ap, -1.0, maskadd4, ALU.mult, ALU.add, [ng, cst], [ar_])

            def G_act(c):
                ar_ = argb[c % 2]; D_ = Dm[c % 2]
                for h in range(4):
                    act(D_.ap[:, h * 128:(h + 1) * 128], ar_.ap[:, h, :], AF.Exp, [ar_, colf], [D_], bias=colf.ap[:, c, h:h + 1])

            def A_pe1(c):
                cs = csl(c); gci = ps_i * 4 + c; cbr = Cb[gci % 2]
                sp_ = nb(); CK[("sp", c)] = sp_
                for h in range(4):
                    for dc in range(2):
                        ch = 2 * h + dc
                        mm(sp_.ap[:, h * 128:(h + 1) * 128], kT[ch].ap[:, cs], qT[ch].ap[:, cs], dc == 0, dc == 1, [kT[ch], qT[ch]], sp_,
                           inc=(h == 3 and dc == 1))
                P1s = []
                for h in range(4):
                    P1 = nb(); P1s.append(P1)
                    for dc in range(2):
                        mm(P1.ap[:, 0:257], qT[2 * h + dc].ap[:, cs], cbr.ap[:, h, dc, 0:257], dc == 0, dc == 1, [qT[2 * h + dc], cbr], P1)
                CK[("P1", c)] = P1s

            def A_St(c):
                tt(Stb[c % 2].ap, CK[("sp", c)].ap, Dm[c % 2].ap, ALU.mult, [CK[("sp", c)], Dm[c % 2]], [Stb[c % 2]])

            def A_copies(c):
                nd = numdb[c % 2]
                for h in range(4):
                    P1 = CK[("P1", c)][h]
                    act(nd.ap[:, h, :], P1.ap[:, 0:257], AF.Copy, [P1, sm4], [nd], scale=winter[:, c, h:h + 1])

            def A_P2(c):
                St_ = Stb[c % 2]; P2s = []
                for h in range(4):
                    P2 = nb(); P2s.append(P2)
                    mm(P2.ap[:, 0:257], St_.ap[:, h * 128:(h + 1) * 128], vaug.ap[:, c, h, 0:257], True, True, [St_, vaug], P2)
                CK[("P2", c)] = P2s

            def A_add(c):
                nd = numdb[c % 2]
                for h in range(4):
                    P2 = CK[("P2", c)][h]
                    tt(nd.ap[:, h, :], nd.ap[:, h, :], P2.ap[:, 0:257], ALU.add, [nd, P2], [nd])

            def B_state(c):
                gci = ps_i * 4 + c; cbw = Cb[(gci + 1) % 2]
                for h in range(4):
                    for dc in range(2):
                        U = nb()
                        mm(U.ap[:, 0:257], kw.ap[:, c, h * 256 + dc * 128: h * 256 + (dc + 1) * 128], vaug.ap[:, c, h, 0:257], True, True, [kw, vaug], U)
                        stt(Cst.ap[:, h, dc, :], Cst.ap[:, h, dc, :], wcb[:, c, h:h + 1], U.ap[:, 0:257], ALU.mult, ALU.add, [Cst, sm4, U], [Cst])
                cp("act", cbw.ap[:, :, :, 0:257], Cst.ap, [Cst], [cbw])

            def T_abs(c):
                act(dn, numdb[c % 2].ap[:, :, 256], AF.Abs, [numdb[c % 2]], [st_dn])

            def T_maxrecip(c):
                tt(dn, dn, nrm[:, c, :], ALU.max, [st_dn, sm4], [st_dn])
                S.op("dve", lambda e: e.reciprocal(out=rden, in_=dn), reads=[st_dn], writes=[st_dn])

            def T_hh(c):
                nd = numdb[c % 2]
                for h in range(4):
                    act(nd.ap[:, h, 0:256], nd.ap[:, h, 0:256], AF.Copy, [nd, st_dn], [nd], scale=rden[:, h:h + 1])

            def T_bn(c):
                nd = numdb[c % 2]
                for h in range(4):
                    S.op("dve", lambda e, h=h: e.bn_stats(out=bst[:, h, :], in_=nd.ap[:, h, 0:256]), reads=[nd], writes=[st_bn])
                    S.op("dve", lambda e, h=h: e.bn_aggr(out=mv[:, h, :], in_=bst[:, h, :]), reads=[st_bn], writes=[st_bn])

            def T_sqrt(c):
                act(sd, mv[:, :, 1], AF.Sqrt, [st_bn, smallc], [st_rs], bias=eps5c)

            def T_norm(c):
                nd = numdb[c % 2]
                S.op("dve", lambda e: e.reciprocal(out=rs, in_=sd), reads=[st_rs], writes=[st_rs])
                for h in range(4):
                    ts(nd.ap[:, h, 0:256], nd.ap[:, h, 0:256], mv[:, h, 0:1], rs[:, h:h + 1], ALU.subtract, ALU.mult, [nd, st_bn, st_rs], [nd])

            def T_out(c):
                nd = numdb[c % 2]; cs = csl(c)
                warm(WK_ML if c < 3 else WK_ML + 22)
                for half in range(2):
                    tb_ = nb()
                    for j4 in range(4):
                        ch = half * 4 + j4
                        h, ec = ch // 2, ch % 2
                        tr(tb_.ap[:, j4 * 128:(j4 + 1) * 128], nd.ap[:, h, ec * 128:(ec + 1) * 128], ident, [nd, cst], tb_, inc=(j4 == 3))
                    for j4 in range(4):
                        ch = half * 4 + j4
                        stt(yain[ch].ap[:, cs], tb_.ap[:, j4 * 128:(j4 + 1) * 128], mngc[:, ch:ch + 1], sigo[ch].ap[:, cs],
                            ALU.mult, ALU.mult, [tb_, cols, sigo[ch]], [yain[ch]])

            G_dve1(0); warm(12); G_pe(0); G_dve2(0); G_act(0)
            for c in range(4):
                p = c - 1
                n = c + 1 if c + 1 < 4 else None
                if p >= 0:
                    T_abs(p)
                if n is not None:
                    G_dve1(n); G_pe(n)
                if p >= 0:
                    T_maxrecip(p)
                if n is not None:
                    G_dve2(n); G_act(n)
                if p >= 0:
                    T_hh(p)
                A_pe1(c)
                A_St(c)
                A_copies(c)
                A_P2(c)
                if p >= 0:
                    T_bn(p)
                A_add(c)
                if p >= 0:
                    T_sqrt(p)
                B_state(c)
                if p >= 0:
                    T_norm(p); T_out(p)
            T_abs(3); T_maxrecip(3); T_hh(3); T_bn(3); T_sqrt(3); T_norm(3); T_out(3)
            st8.w = list(st_dn.w) + list(st_rs.w) + list(st_bn.w)
            st8.r = list(st_dn.r) + list(st_rs.r) + list(st_bn.r)
            cp("act", Bc.ap[:, 0:1], Bc.ap[:, PW:PW + 1], [Bc], [Bc])
            cp("act", Gt.ap[:, 0:1], Gt.ap[:, PW:PW + 1], [Gt], [Gt])

            if use_s:
                S.barrier()
                ar_off[0] = mark_ml
                SC = slice(PW, PW + NS)
                C0f = [carve(f"C0f{i}", [128, 8, 256]) for i in range(2)]
                C0b = Buf("C0b", kw.ap.rearrange("p a b -> p (a b)")[:, 0:2048].rearrange("p (c e) -> p c e", c=8), arena=True)
                n0T = carve("n0T", [128, 128]); rowsN = carve("rowsN", [128, 128])
                m0c = carve("m0c", [4, NS]); t_int = carve("t_int", [4, NS]); t_mn = carve("t_mn", [4, NS])
                Qs = carve("Qs", [4, 3, NS]); BmQ = carve("BmQ", [4, 3, 4, NS]); sbq = carve("sbq", [128, 3, 4, NS])
                qk = carve("qk", [128, 4, NS]); qn = carve("qn", [128, 4, NS]); s_ = carve("s_", [128, 4, NS])
                den = carve("den", [128, 4, NS]); rdn = carve("rdn", [128, 4, NS])
                mean_s = carve("mean_s", [128, 4, NS]); var_s = carve("var_s", [128, 4, NS])
                hq = carve("hq", [128, 8, NS]); numT = carve("numT", [128, 8, NS]); kws = carve("kws", [128, 8, NS])
                nnew = carve("nnew", [128, 8, NS]); t8 = carve("t8", [128, 8, NS])
                vm = carve("vm", [NS, 256])
                vtok_s = xin[0]; crows = xin[1]
                vtok_ap = xin[0].ap[0:NS, :]; crows_ap = xin[1].ap[0:2 * NS, :]
                w_int_b = sbq.ap[:, 0]; es_b = sbq.ap[:, 1]; nrm_b = sbq.ap[:, 2]
                n0v = n0T.ap.rearrange("p (b c) -> p c b", c=8)
                S.dma("sp", rowsN.ap, sn_d, reads=[wsrc], writes=[rowsN])
                bk = nb(); tr(bk.ap[:, 0:128], rowsN.ap, ident, [rowsN, cst], bk)
                cp("dve", n0T.ap, bk.ap[:, 0:128], [bk], [n0T])
                with nc.allow_non_contiguous_dma(reason="tiny"):
                    S.dma("sp", m0c.ap, sm_d.rearrange("b h -> h b"), reads=[wsrc], writes=[m0c])
                S.dma("sp", crows_ap, scv_d, reads=[wsrc], writes=[crows])
                for half in range(2):
                    bk = nb()
                    for f4 in range(4):
                        fc = half * 4 + f4
                        tr(bk.ap[:, f4 * 32:(f4 + 1) * 32], crows_ap[:, fc * 128:(fc + 1) * 128], ident[0:32, 0:32], [crows, cst], bk, inc=(f4 == 3))
                    cp("dve", cbufT.ap[:, half * 4:(half + 1) * 4, :], bk.ap[:, 0:128].rearrange("p (a b) -> p a b", a=4), [bk], [cbufT])
                ob = dbuf("cvs0")
                S.dma("sp", cvs_d[:, 0, :], scv_d.rearrange("(b j) d -> b j d", j=2)[:, 1, :], reads=[wsrc], writes=[ob]); out_bufs.append(ob)
                for half in range(2):
                    bk = nb()
                    for f4 in range(4):
                        c = half * 4 + f4
                        tr(bk.ap[0:NS, f4 * 128:(f4 + 1) * 128], vTs.ap[:, c, :], ident, [vTs, cst], bk, inc=(f4 == 3))
                    cp("dve", vtok_ap[:, half * 512:(half + 1) * 512], bk.ap[0:NS, :], [bk], [vtok_s])
                tt(t_int.ap, G2.ap[:, SC], m0c.ap, ALU.add, [G2, m0c], [t_int])
                tt(t_mn.ap, t_int.ap, G1.ap[:, SC], ALU.max, [t_int, G1], [t_mn])
                tt(Qs.ap[:, 0, :], t_int.ap, t_mn.ap, ALU.subtract, [t_int, t_mn], [Qs])
                tt(Qs.ap[:, 1, :], G1.ap[:, SC], t_mn.ap, ALU.subtract, [G1, t_mn], [Qs])
                S.op("dve", lambda e: e.tensor_scalar(out=Qs.ap[:, 2, :], in0=t_mn.ap, scalar1=-1.0, scalar2=None, op0=ALU.mult), reads=[t_mn], writes=[Qs])
                act(Qs.ap.rearrange("p a b -> p (a b)"), Qs.ap.rearrange("p a b -> p (a b)"), AF.Exp, [Qs], [Qs])
                with nc.allow_non_contiguous_dma(reason="tiny"):
                    ob = dbuf("msd"); S.dma("sp", ms_d.rearrange("b h -> h b"), t_mn.ap, reads=[t_mn], writes=[ob]); out_bufs.append(ob)
                tt(BmQ.ap, Qs.ap.unsqueeze(2).to_broadcast([4, 3, 4, NS]), eye4.unsqueeze(1).unsqueeze(3).to_broadcast([4, 3, 4, NS]),
                   ALU.mult, [Qs, cst], [BmQ])
                bk = nb()
                mm(bk.ap[:, 0:192], ones4.ap, BmQ.ap.rearrange("p a b c -> p (a b c)"), True, True, [ones4, BmQ], bk)
                cp("act", sbq.ap.rearrange("p a b c -> p (a b c)"), bk.ap[:, 0:192], [bk], [sbq])
                pq = sq_p[0]; pn = sq_p[1]
                for ch in range(8):
                    tt(pq.ap[:, ch * NS:(ch + 1) * NS], qT[ch].ap[:, SC], kT[ch].ap[:, SC], ALU.mult, [qT[ch], kT[ch]], [pq])
                    tt(pn.ap[:, ch * NS:(ch + 1) * NS], qT[ch].ap[:, SC], n0v[:, ch, :], ALU.mult, [qT[ch], n0T], [pn])
                b1 = nb(); b2 = nb()
                mm(b1.ap[:, 0:128], ones_r.ap, pq.ap[:, 0:128], True, True, [ones_r, pq], b1)
                mm(b2.ap[:, 0:128], ones_r.ap, pn.ap[:, 0:128], True, True, [ones_r, pn], b2)
                v1 = b1.ap[:, 0:128].rearrange("p (h d b) -> p h d b", h=4, d=2)
                v2 = b2.ap[:, 0:128].rearrange("p (h d b) -> p h d b", h=4, d=2)
                cp("act", qk.ap, v1[:, :, 0, :], [b1], [qk])
                tt(qk.ap, qk.ap, v1[:, :, 1, :], ALU.add, [qk, b1], [qk])
                cp("act", qn.ap, v2[:, :, 0, :], [b2], [qn])
                tt(qn.ap, qn.ap, v2[:, :, 1, :], ALU.add, [qn, b2], [qn])
                tt(s_.ap, qk.ap, es_b, ALU.mult, [qk, sbq], [s_])
                tt(den.ap, qn.ap, w_int_b, ALU.mult, [qn, sbq], [den])
                tt(den.ap, den.ap, s_.ap, ALU.add, [den, s_], [den])
                act(den.ap, den.ap, AF.Abs, [den], [den])
                tt(den.ap, den.ap, nrm_b, ALU.max, [den, sbq], [den])
                S.op("dve", lambda e: e.reciprocal(out=rdn.ap, in_=den.ap), reads=[den], writes=[rdn])
                for ch in range(8):
                    h = ch // 2
                    tt(kws.ap[:, ch, :], kT[ch].ap[:, SC], es_b[:, h, :], ALU.mult, [kT[ch], sbq], [kws])
                    tt(nnew.ap[:, ch, :], n0v[:, ch, :], w_int_b[:, h, :], ALU.mult, [n0T, sbq], [nnew])
                tt(nnew.ap, nnew.ap, kws.ap, ALU.add, [nnew, kws], [nnew])
                nsrows = xin[1]; nsrows_ap = xin[1].ap[0:NS, :]
                for half in range(2):
                    bk = nb()
                    for f4 in range(4):
                        c = half * 4 + f4
                        tr(bk.ap[0:NS, f4 * 128:(f4 + 1) * 128], nnew.ap[:, c, :], ident, [nnew, cst], bk, inc=(f4 == 3))
                    cp("dve", nsrows_ap[:, half * 512:(half + 1) * 512], bk.ap[0:NS, :], [bk], [nsrows])
                ob = dbuf("nsd"); S.dma("sp", ns_d, nsrows_ap, reads=[nsrows], writes=[ob]); out_bufs.append(ob)
                hqb = nb(); reserved.add(hqb.name)
                cfh = [[Buf(f"cfh{i}{k}", C0f[i].ap[:, 4 * k:4 * k + 4, :], arena=True) for k in range(2)] for i in range(2)]
                for i in range(2):
                    for k in range(2):
                        cfh[i][k].w = list(C0f[i].w)
                C0bh = [Buf(f"C0b{k}", C0b.ap[:, 4 * k:4 * k + 4, :], arena=True) for k in range(2)]
                for k in range(2):
                    C0bh[k].w = list(C0b.w)

                def load_c0(b_, k):
                    cf_ = cfh[b_ % 2][k]
                    S.dma("sp", cf_.ap.rearrange("p (h d) e -> p h d e", h=2), sC_d[b_, 2 * k:2 * k + 2].rearrange("h (d p) e -> p h d e", p=128),
                          reads=[wsrc], writes=[cf_])

                def vm_loc(bb, h):
                    p0 = 32 * (h // 2)
                    if bb % 2 == 0:
                        return pin[h % 2], pin[h % 2].ap[p0:p0 + NS, :], p0
                    return xin[1], xin[1].ap[p0:p0 + NS, (h % 2) * 256:(h % 2 + 1) * 256], p0

                def emit_vb(bb):
                    vbs_ = [nb(), nb()]
                    for h in range(4):
                        buf_, ap_, p0 = vm_loc(bb, h)
                        ts(ap_, vtok_ap[:, h * 256:(h + 1) * 256], ident[0:NS, bb:bb + 1], None, ALU.mult, ALU.bypass, [vtok_s, cst], [buf_])
                    for h in range(4):
                        buf_, ap_, p0 = vm_loc(bb, h)
                        vb = vbs_[h // 2]
                        mm(vb.ap[:, (h % 2) * 256:(h % 2 + 1) * 256], tmp1.ap[p0:p0 + NS, :], ap_, True, True, [tmp1, buf_], vb)
                    return vbs_

                load_c0(0, 0); load_c0(0, 1)
                vb_next = emit_vb(0)
                for b in range(NS):
                    if b + 1 < NS:
                        load_c0(b + 1, 0); load_c0(b + 1, 1)
                    vbs = vb_next
                    for k in range(2):
                        cf = cfh[b % 2][k]; cb_ = C0bh[k]
                        cp("act", cb_.ap, cf.ap, [cf], [cb_])
                        for hh_ in range(2):
                            h = 2 * k + hh_
                            for ec in range(2):
                                col = (h * 2 + ec) * NS + b
                                for dc in range(2):
                                    mm(hqb.ap[:, col:col + 1], cb_.ap[:, hh_ * 2 + dc, ec * 128:(ec + 1) * 128], qT[h * 2 + dc].ap[:, PW + b:PW + b + 1],
                                       dc == 0, dc == 1, [cb_, qT[h * 2 + dc]], hqb, inc=(dc == 1 and ec == 1 and hh_ == 1))
                        if k == 0 and b + 1 < NS:
                            vb_next = emit_vb(b + 1)
                        for hh_ in range(2):
                            h = 2 * k + hh_
                            act(cf.ap[:, 2 * hh_:2 * hh_ + 2, :], cf.ap[:, 2 * hh_:2 * hh_ + 2, :], AF.Copy, [cf, sbq], [cf], scale=w_int_b[:, h, b:b + 1])
                        vb = vbs[k]
                        for hh_ in range(2):
                            h = 2 * k + hh_
                            for dc in range(2):
                                ch = h * 2 + dc
                                stt(cf.ap[:, hh_ * 2 + dc, :], vb.ap[:, hh_ * 256:(hh_ + 1) * 256], kws.ap[:, ch, b:b + 1], cf.ap[:, hh_ * 2 + dc, :], ALU.mult, ALU.add, [vb, kws, cf], [cf])
                        ob = dbuf(f"Csd{b}_{k}")
                        S.dma("sp", Cs_d[b, 2 * k:2 * k + 2].rearrange("h (d p) e -> p h d e", p=128), cf.ap.rearrange("p (h d) e -> p h d e", h=2),
                              reads=[cf], writes=[ob]); out_bufs.append(ob)
                cp("act", hq.ap.rearrange("p a b -> p (a b)"), hqb.ap[:, 0:128], [hqb], [hq])
                reserved.discard(hqb.name)
                for ch in range(8):
                    h = ch // 2
                    tt(numT.ap[:, ch, :], hq.ap[:, ch, :], w_int_b[:, h, :], ALU.mult, [hq, sbq], [numT])
                    tt(t8.ap[:, ch, :], vTs.ap[:, ch, :], s_.ap[:, h, :], ALU.mult, [vTs, s_], [t8])
                tt(numT.ap, numT.ap, t8.ap, ALU.add, [numT, t8], [numT])
                for ch in range(8):
                    tt(numT.ap[:, ch, :], numT.ap[:, ch, :], rdn.ap[:, ch // 2, :], ALU.mult, [numT, rdn], [numT])
                hr = xr_p[0]; hs = xr_p[1]
                cp("dve", hr.ap[:, 0:128], numT.ap.rearrange("p a b -> p (a b)"), [numT], [hr])
                tt(hs.ap[:, 0:128], numT.ap.rearrange("p a b -> p (a b)"), numT.ap.rearrange("p a b -> p (a b)"), ALU.mult, [numT], [hs])
                b1 = nb(); b2 = nb()
                mm(b1.ap[:, 0:128], ones_r.ap, hr.ap[:, 0:128], True, True, [ones_r, hr], b1)
                mm(b2.ap[:, 0:128], ones_r.ap, hs.ap[:, 0:128], True, True, [ones_r, hs], b2)
                v1 = b1.ap[:, 0:128].rearrange("p (h d b) -> p h d b", h=4, d=2)
                v2 = b2.ap[:, 0:128].rearrange("p (h d b) -> p h d b", h=4, d=2)
                cp("act", mean_s.ap, v1[:, :, 0, :], [b1], [mean_s])
                tt(mean_s.ap, mean_s.ap, v1[:, :, 1, :], ALU.add, [mean_s, b1], [mean_s])
                cp("act", var_s.ap, v2[:, :, 0, :], [b2], [var_s])
                tt(var_s.ap, var_s.ap, v2[:, :, 1, :], ALU.add, [var_s, b2], [var_s])
                S.op("dve", lambda e: e.tensor_scalar(out=mean_s.ap, in0=mean_s.ap, scalar1=1.0 / 256.0, scalar2=None, op0=ALU.mult), reads=[mean_s], writes=[mean_s])
                tt(qk.ap, mean_s.ap, mean_s.ap, ALU.mult, [mean_s], [qk])
                stt(var_s.ap, var_s.ap, 1.0 / 256.0, qk.ap, ALU.mult, ALU.subtract, [var_s, qk], [var_s])
                act(var_s.ap, var_s.ap, AF.Sqrt, [var_s, smallc], [var_s], bias=eps5c)
                S.op("dve", lambda e: e.reciprocal(out=var_s.ap, in_=var_s.ap), reads=[var_s], writes=[var_s])
                for ch in range(8):
                    h = ch // 2
                    tt(numT.ap[:, ch, :], numT.ap[:, ch, :], mean_s.ap[:, h, :], ALU.subtract, [numT, mean_s], [numT])
                    tt(numT.ap[:, ch, :], numT.ap[:, ch, :], var_s.ap[:, h, :], ALU.mult, [numT, var_s], [numT])
                    stt(yain[ch].ap[:, SC], numT.ap[:, ch, :], mngc[:, ch:ch + 1], sigo[ch].ap[:, SC], ALU.mult, ALU.mult,
                        [numT, cols, sigo[ch]], [yain[ch]])

            S.barrier()
            ar_off[0] = mark2b
            ybin = [Buf(f"ybin{i}", kT[i].ap) for i in range(8)]
            mmix = [Buf(f"mmix{i}", qT[i].ap) for i in range(8)]
            Cc = [carve(f"Cc{i}", [128, NTM]) for i in range(2)]
            pre = [carve(f"pre{i}", [128, NTM + 2]) for i in range(2)]
            acc = [carve(f"acc{i}", [128, NTM]) for i in range(2)]
            for g2 in range(2):
                sC_ = next_slot(); vC = sC_.ap.rearrange("p (kc n) -> p kc n", kc=8)
                wload(sC_, None, wview(w_in_d, CC0 + g2 * 512, 512), view=V8)
                sH_ = next_slot(); vH = sH_.ap.rearrange("p (kc n) -> p kc n", kc=8)
                wload(sH_, None, wview(w_in_d, CH0 + g2 * 512, 512), view=V8)
                sB_ = next_slot(); vB = sB_.ap.rearrange("p (kc n) -> p kc n", kc=8)
                wload(sB_, None, wview(w_in_d, CB0 + g2 * 512, 512), view=V8)
                for ci in range(4):
                    fc = g2 * 4 + ci
                    Cc_ = Cc[fc % 2]; pre_ = pre[fc % 2]; acc_ = acc[fc % 2]
                    cp("act", pre_.ap[:, 0:2], ccar.ap[:, fc, :], [ccar], [pre_])
                    for (n0, n) in ctiles:
                        bC = nb(); bH = nb(); bB = nb()
                        for kc in range(8):
                            mm(bC.ap[:, 0:n], vC[:, kc, ci * 128:(ci + 1) * 128], xb[kc].ap[:, n0:n0 + n], kc == 0, kc == 7, [sC_, xb[kc]], bC)
                        for kc in range(8):
                            mm(bH.ap[:, 0:n], vH[:, kc, ci * 128:(ci + 1) * 128], xb[kc].ap[:, n0:n0 + n], kc == 0, kc == 7, [sH_, xb[kc]], bH)
                        for kc in range(8):
                            mm(bB.ap[:, 0:n], vB[:, kc, ci * 128:(ci + 1) * 128], xb[kc].ap[:, n0:n0 + n], kc == 0, kc == 7, [sB_, xb[kc]], bB)
                        act(Cc_.ap[:, n0:n0 + n], bC.ap[:, 0:n], AF.Identity, [bC, cols], [Cc_], bias=bcol[:, 40 + fc:41 + fc])
                        stt(pre_.ap[:, 2 + n0:2 + n0 + n], bH.ap[:, 0:n], bcol[:, 48 + fc:49 + fc], Cc_.ap[:, n0:n0 + n], ALU.add, ALU.mult,
                            [bH, cols, Cc_], [pre_])
                        if n0 == 0:
                            act(acc_.ap[:, 0:n], pre_.ap[:, 2:2 + n], AF.Identity, [pre_, cols], [acc_], bias=cbc[:, fc:fc + 1], scale=cwc[:, 16 + fc:17 + fc])
                            stt(acc_.ap[:, 0:n], pre_.ap[:, 1:1 + n], cwc[:, 8 + fc:9 + fc], acc_.ap[:, 0:n], ALU.mult, ALU.add, [pre_, cols, acc_], [acc_])
                            stt(acc_.ap[:, 0:n], pre_.ap[:, 0:n], cwc[:, fc:fc + 1], acc_.ap[:, 0:n], ALU.mult, ALU.add, [pre_, cols, acc_], [acc_])
                            cp("act", ccar.ap[:, fc, :], pre_.ap[:, n:n + 2], [pre_], [ccar])
                        else:
                            ps_ = pre_.ap[:, 2 + n0:2 + n0 + n]
                            act(acc_.ap[:, n0:n0 + n], ps_, AF.Identity, [pre_, cols], [acc_], bias=cbc[:, fc:fc + 1], scale=cwc[:, 16 + fc:17 + fc])
                            cbv = cbufT.ap[:, fc, :].rearrange("p (b j) -> p j b", j=2)
                            stt(acc_.ap[:, n0:n0 + n], cbv[:, 1, :], cwc[:, 8 + fc:9 + fc], acc_.ap[:, n0:n0 + n], ALU.mult, ALU.add, [cbufT, cols, acc_], [acc_])
                            stt(acc_.ap[:, n0:n0 + n], cbv[:, 0, :], cwc[:, fc:fc + 1], acc_.ap[:, n0:n0 + n], ALU.mult, ALU.add, [cbufT, cols, acc_], [acc_])
                            cp("act", pres_all.ap[:, fc, :], ps_, [pre_], [pres_all])
                        stt(ybin[fc].ap[:, n0:n0 + n], bB.ap[:, 0:n], bcol[:, 32 + fc:33 + fc], acc_.ap[:, n0:n0 + n], ALU.add, ALU.mult,
                            [bB, cols, acc_], [ybin[fc]])
            S.barrier()
            ar_off[0] = mark2b
            sga = [carve(f"sga{i}", [128, NTM]) for i in range(2)]
            ta = [carve(f"ta{i}", [128, NTM]) for i in range(2)]
            tb = [carve(f"tb{i}", [128, NTM]) for i in range(2)]
            for g4 in range(4):
                s1_ = next_slot(); v1 = s1_.ap.rearrange("p (w kc n) -> p w kc n", w=2, kc=8)
                wload(s1_, None, wview(w_in_d, GA0 + g4 * 256, 256), view=lambda a: a.rearrange("p (w kc n) -> p w kc n", w=2, kc=8)[:, 0])
                wload(s1_, None, wview(w_a_d, g4 * 256, 256), join=True, view=lambda a: a.rearrange("p (w kc n) -> p w kc n", w=2, kc=8)[:, 1])
                s2_ = next_slot(); v2 = s2_.ap.rearrange("p (w kc n) -> p w kc n", w=2, kc=8)
                wload(s2_, None, wview(w_in_d, GB0 + g4 * 256, 256), view=lambda a: a.rearrange("p (w kc n) -> p w kc n", w=2, kc=8)[:, 0])
                wload(s2_, None, wview(w_b_d, g4 * 256, 256), join=True, view=lambda a: a.rearrange("p (w kc n) -> p w kc n", w=2, kc=8)[:, 1])
                for ci in range(2):
                    dc = g4 * 2 + ci
                    for (n0, n) in ctiles:
                        k_ = dc % 2
                        bga = nb(); bya = nb(); bgb = nb(); byb = nb()
                        for kc in range(8):
                            mm(bga.ap[:, 0:n], v1[:, 0, kc, ci * 128:(ci + 1) * 128], xb[kc].ap[:, n0:n0 + n], kc == 0, kc == 7, [s1_, xb[kc]], bga)
                        for kc in range(8):
                            mm(bya.ap[:, 0:n], v1[:, 1, kc, ci * 128:(ci + 1) * 128], yain[kc].ap[:, n0:n0 + n], kc == 0, kc == 7, [s1_, yain[kc]], bya)
                        for kc in range(8):
                            mm(bgb.ap[:, 0:n], v2[:, 0, kc, ci * 128:(ci + 1) * 128], xb[kc].ap[:, n0:n0 + n], kc == 0, kc == 7, [s2_, xb[kc]], bgb)
                        for kc in range(8):
                            mm(byb.ap[:, 0:n], v2[:, 1, kc, ci * 128:(ci + 1) * 128], ybin[kc].ap[:, n0:n0 + n], kc == 0, kc == 7, [s2_, ybin[kc]], byb)
                        act(sga[k_].ap[:, 0:n], bga.ap[:, 0:n], AF.Sigmoid, [bga, cols], [sga[k_]], bias=bcol[:, 56 + dc:57 + dc])
                        tt(ta[k_].ap[:, 0:n], sga[k_].ap[:, 0:n], bya.ap[:, 0:n], ALU.mult, [sga[k_], bya], [ta[k_]])
                        act(sga[k_].ap[:, 0:n], bgb.ap[:, 0:n], AF.Sigmoid, [bgb, cols], [sga[k_]], bias=bcol[:, 64 + dc:65 + dc])
                        tt(tb[k_].ap[:, 0:n], sga[k_].ap[:, 0:n], byb.ap[:, 0:n], ALU.mult, [sga[k_], byb], [tb[k_]])
                        tt(mmix[dc].ap[:, n0:n0 + n], ta[k_].ap[:, 0:n], tb[k_].ap[:, 0:n], ALU.add, [ta[k_], tb[k_]], [mmix[dc]])
            ln_begin(ctiles)
            for g2 in range(2):
                sl = next_slot(); v = sl.ap.rearrange("p (kc n) -> p kc n", kc=8)
                wload(sl, None, wview(w_mix_d, g2 * 512, 512), view=V8)

                def ev_m(ci, ti, nn, bk, g2=g2):
                    dc = g2 * 4 + ci
                    sl_ = slice(nn[0], nn[0] + nn[1])
                    stt(xf[dc].ap[:, sl_], bk.ap[:, 0:nn[1]], IA, xf[dc].ap[:, sl_], ALU.mult, ALU.add, [bk, xf[dc]], [xf[dc]])
                    ln_acc(dc, ti, nn)
                proj(v, 4, mmix, ctiles, sl, ev_m)
            S.barrier()
            ar_off[0] = mark2
            layer_norm(1, ctiles, NT)
            ffn(f2i_d, f2o_d, ctiles, NT)
            layer_norm(2, ctiles, NT)
            mark5 = ar_off[0]
            sgp = [carve(f"sgp{i}", [128, NTM]) for i in range(2)]
            tp = [carve(f"tp{i}", [128, NTM]) for i in range(2)]
            BLK5 = {}
            spp = next_slot(); vpp = spp.ap[:, 0:2048].rearrange("p (kc n) -> p kc n", kc=2)
            wload(spp, None, w_pp_d.rearrange("(kc p) n -> p kc n", p=128), view=lambda a: a[:, 0:2048].rearrange("p (kc n) -> p kc n", kc=2))
            for g2 in range(2):
                sl = next_slot(); v = sl.ap.rearrange("p (kc n) -> p kc n", kc=8)
                wload(sl, None, wview(w_pg_d, g2 * 512, 512), view=V8)
                for ci in range(4):
                    dc = g2 * 4 + ci
                    for ti5, (n0, n) in enumerate(ctiles):
                        k_ = dc % 2
                        if ci % 2 == 0:
                            blk = {}
                            for c2 in range(2):
                                blk[("p", ci + c2)] = nb()
                                for kc in range(2):
                                    d2 = g2 * 4 + ci + c2
                                    mm(blk[("p", ci + c2)].ap[:, 0:n], vpp[:, kc, d2 * 128:(d2 + 1) * 128], pT[kc].ap[:, n0:n0 + n], kc == 0, kc == 1, [spp, pT[kc]], blk[("p", ci + c2)])
                            for c2 in range(2):
                                blk[("g", ci + c2)] = nb()
                            for kc in range(8):
                                for c2 in range(2):
                                    mm(blk[("g", ci + c2)].ap[:, 0:n], v[:, kc, (ci + c2) * 128:(ci + c2 + 1) * 128], xb[kc].ap[:, n0:n0 + n], kc == 0, kc == 7, [sl, xb[kc]], blk[("g", ci + c2)])
                            BLK5[ti5] = blk
                        bg = BLK5[ti5][("g", ci)]; bp = BLK5[ti5][("p", ci)]
                        act(sgp[k_].ap[:, 0:n], bg.ap[:, 0:n], AF.Sigmoid, [bg], [sgp[k_]])
                        tt(tp[k_].ap[:, 0:n], sgp[k_].ap[:, 0:n], bp.ap[:, 0:n], ALU.mult, [sgp[k_], bp], [tp[k_]])
                        stt(xf[dc].ap[:, n0:n0 + n], tp[k_].ap[:, 0:n], IA, xf[dc].ap[:, n0:n0 + n], ALU.mult, ALU.add, [tp[k_], xf[dc]], [xf[dc]])
                        ln_acc(dc, ti5, (n0, n))
            ln_begin(ctiles)
            arena_reset(mark5)
            layer_norm(3, ctiles, NT)
        for tt_i in range(4):
            xo = xin[tt_i % 2]
            for half in range(2):
                bk = nb()
                for f4 in range(4):
                    fc = half * 4 + f4
                    tr(bk.ap[:, f4 * 128:(f4 + 1) * 128], xf[fc].ap[:, tt_i * 128:(tt_i + 1) * 128], ident, [xf[fc], cst], bk, inc=(f4 == 3))
                cp("act" if half == 0 else "dve", xo.ap[:, half * 512:(half + 1) * 512], bk.ap, [bk], [xo])
            yb_ = dbuf("yd")
            S.dma("sp", y_d[t0 + tt_i * 128:t0 + (tt_i + 1) * 128, :], xo.ap, reads=[xo], writes=[yb_])
            out_bufs.append(yb_)
        if use_s:
            xo = xin[0]
            for half in range(2):
                bk = nb()
                for f4 in range(4):
                    fc = half * 4 + f4
                    tr(bk.ap[0:NS, f4 * 128:(f4 + 1) * 128], xf[fc].ap[:, PW:PW + NS], ident, [xf[fc], cst], bk, inc=(f4 == 3))
                cp("act" if half == 0 else "dve", xo.ap[0:NS, half * 512:(half + 1) * 512], bk.ap[0:NS, :], [bk], [xo])
            yb_ = dbuf("ysd")
            S.dma("sp", ys_d, xo.ap[0:NS, :], reads=[xo], writes=[yb_])
            out_bufs.append(yb_)

    if stage < 3:
        S.finish(out_bufs, "sp")
        return nc, S
    ob = dbuf("Cpd")
    S.dma("sp", Cp_d.rearrange("h (dc p) e -> p h dc e", p=128), Cst.ap[:, :, :, 0:256], reads=[Cst], writes=[ob]); out_bufs.append(ob)
    nrow = Buf("nrow", sb("nrow", [8, 128]))
    ncol_ = Buf("ncol", sb("ncol", [128, 8]))
    cp("dve", ncol_.ap, Cst.ap[:, :, :, 256].rearrange("p a b -> p (a b)"), [Cst], [ncol_])
    bk = nb()
    tr(bk.ap[0:8, 0:128], ncol_.ap, ident, [ncol_, cst], bk)
    cp("dve", nrow.ap, bk.ap[0:8, 0:128], [bk], [nrow])
    ob = dbuf("npd"); S.dma("sp", np_d, nrow.ap, reads=[nrow], writes=[ob]); out_bufs.append(ob)
    mfin = Buf("mfin", sb("mfin", [4, 1]))
    tt(mfin.ap, Bc.ap[:, 0:1], Gt.ap[:, 0:1], ALU.add, [Bc, Gt], [mfin])
    with nc.allow_non_contiguous_dma(reason="tiny"):
        ob = dbuf("mpd"); S.dma("sp", mp_d, mfin.ap, reads=[mfin], writes=[ob]); out_bufs.append(ob)
    crow = Buf("crow", sb("crow", [2, D]))
    for half in range(2):
        bk = nb()
        for f4 in range(4):
            fc = half * 4 + f4
            tr(bk.ap[0:2, f4 * 128:(f4 + 1) * 128], ccar.ap[:, fc, :], ident, [ccar, cst], bk, inc=(f4 == 3))
        cp("dve", crow.ap[:, half * 512:(half + 1) * 512], bk.ap[0:2, :], [bk], [crow])
    ob = dbuf("cvpd"); S.dma("sp", cvp_d, crow.ap, reads=[crow], writes=[ob]); out_bufs.append(ob)
    if with_samples:
        prow = xin[1]; prow_ap = xin[1].ap[0:NS, :]
        for half in range(2):
            bk = nb()
            for f4 in range(4):
                fc = half * 4 + f4
                tr(bk.ap[0:NS, f4 * 128:(f4 + 1) * 128], pres_all.ap[:, fc, :], ident, [pres_all, cst], bk, inc=(f4 == 3))
            cp("dve", prow_ap[:, half * 512:(half + 1) * 512], bk.ap[0:NS, :], [bk], [prow])
        ob = dbuf("cvs1"); S.dma("sp", cvs_d[:, 1, :], prow_ap, reads=[prow], writes=[ob]); out_bufs.append(ob)
    S.finish(out_bufs, "sp")
    return nc, S


EXTRA_OUT = []
AR_HW = [0]


def build_samples(L):
    raise NotImplementedError


def conv_samples(L):
    raise NotImplementedError


_CACHE = {}


def _consts():
    c = np.zeros((128, 768), np.float32)
    c[:, 0:128] = np.eye(128, dtype=np.float32)
    s = np.arange(128)[:, None]
    t = np.arange(128)[None, :]
    m = np.where(t >= s, 0.0, -1e30).astype(np.float32)
    c[:, 128:640] = np.tile(m, (1, 4))
    c[0:4, 640:644] = np.eye(4, dtype=np.float32)
    return c


def kernel(x_prompt, x_sample, p_prompt, p_sample, state_C, state_n, state_m, state_conv,
           w_in, b_in, m_norm_g, w_a, w_b, conv_w, conv_b, w_mix,
           ffn1_wi, ffn1_wo, ffn2_wi, ffn2_wo, w_pg, w_pp, ln_g, ln_b):
    f = lambda a: np.ascontiguousarray(np.asarray(a, dtype=np.float32))
    if "nc" not in _CACHE:
        _CACHE["nc"] = build_program(with_samples=WITH_SAMPLES)[0]
    nc = _CACHE["nc"]
    shared = {
        "w_in": f(w_in[0]), "b_in": f(b_in[0]).reshape(1, NIN), "m_norm_g": f(m_norm_g[0]).reshape(8, 128),
        "w_a": f(w_a[0]), "w_b": f(w_b[0]), "conv_w": f(conv_w[0]).reshape(24, 128), "conv_b": f(conv_b[0]).reshape(8, 128),
        "w_mix": f(w_mix[0]), "ffn1_wi": f(ffn1_wi[0]), "ffn1_wo": f(ffn1_wo[0]), "ffn2_wi": f(ffn2_wi[0]),
        "ffn2_wo": f(ffn2_wo[0]), "w_pg": f(w_pg[0]), "w_pp": f(w_pp[0]), "ln_g": f(ln_g[0]).reshape(32, 128),
        "ln_b": f(ln_b[0]).reshape(32, 128), "consts": _consts(),
    }
    in_maps = []
    for b in range(8):
        sl = slice(NS * b, NS * (b + 1))
        m = dict(shared)
        m.update({
            "x": f(x_prompt[b]), "xs": f(x_sample[sl, 0]), "p": f(p_prompt[0, b]), "psm": f(p_sample[0, sl, 0]),
            "sC": f(state_C[0, sl]), "sn": f(state_n[0, sl]).reshape(NS * 8, 128), "sm": f(state_m[0, sl]),
            "scv": f(state_conv[0, sl]).reshape(NS * 2, D),
        })
        in_maps.append(m)
    res = run_bass_kernel_spmd(nc, in_maps, core_ids=list(range(8)))
    R = res.results
    y = np.stack([R[b]["y"] for b in range(8)])
    ys = np.concatenate([R[b]["ys"] for b in range(8)])[:, None, :]
    Cp = np.stack([R[b]["Cp"] for b in range(8)])[None]
    npr = np.stack([R[b]["np"].reshape(4, 256) for b in range(8)])[None]
    mp = np.stack([R[b]["mp"].reshape(4) for b in range(8)])[None]
    cvp = np.stack([R[b]["cvp"] for b in range(8)])[None]
    Cs = np.concatenate([R[b]["Cs"] for b in range(8)])[None]
    ns = np.concatenate([R[b]["ns"].reshape(NS, 4, 256) for b in range(8)])[None]
    ms = np.concatenate([R[b]["ms"] for b in range(8)])[None]
    cvs = np.concatenate([R[b]["cvs"] for b in range(8)])[None]
    return (y, ys, Cp, npr, mp, cvp, Cs, ns, ms, cvs)


WITH_SAMPLES = True
```
